# Optimizing a Trainium2 kernel written in Bass

```python
import math
import jax, jax.numpy as jnp
from jax import lax
import numpy as np

D_MODEL = 2048
BATCH = 8
SEQ = 2048
DEPTH = 1

SSM_WIDTH = D_MODEL // 2
SSM_GROUP = 16
SSM_GROUPS = SSM_WIDTH // SSM_GROUP
SSM_STATE = 64
DT_MIN = 0.001
DT_MAX = 0.1
ATTN_HEADS = 8
HEAD_DIM = 128
ATTN_WIDTH = ATTN_HEADS * HEAD_DIM
MOBA_BLOCK = 256
MOBA_TOPK = 3
Q_BLOCK = 128
REL_BUCKETS = 32
REL_MAX_DIST = 128
D_FF = 4 * D_MODEL
N_MOD = 6
EPS = 1e-6
NEG_INF = -1e30
IN_WIDTH = SSM_WIDTH + 3 * ATTN_WIDTH + 2 * D_MODEL

kernel_name = "hybrid_s5_moba_gated_block"


def rms_norm(x, g):
    x32 = x.astype(jnp.float32)
    y = x32 * lax.rsqrt(jnp.mean(x32 * x32, axis=-1, keepdims=True) + EPS)
    return (y * g.astype(jnp.float32)).astype(x.dtype)


def _ssm_combine(lhs, rhs):
    a_re1, a_im1, b_re1, b_im1 = lhs
    a_re2, a_im2, b_re2, b_im2 = rhs
    a_re = a_re2 * a_re1 - a_im2 * a_im1
    a_im = a_re2 * a_im1 + a_im2 * a_re1
    b_re = a_re2 * b_re1 - a_im2 * b_im1 + b_re2
    b_im = a_re2 * b_im1 + a_im2 * b_re1 + b_im2
    return (a_re, a_im, b_re, b_im)


def s5_branch(u, a_re, a_im, log_dt, b_re, b_im, c_re, c_im, d_skip, w_glu, b_glu):
    bsz, seq_len, _ = u.shape
    f32 = jnp.float32
    u32 = u.astype(f32).reshape(bsz, seq_len, SSM_GROUPS, SSM_GROUP)
    a_re = a_re.astype(f32)
    a_im = a_im.astype(f32)
    dt = jnp.exp(log_dt.astype(f32))[:, None]
    mag = jnp.exp(dt * a_re)
    abar_re = mag * jnp.cos(dt * a_im)
    abar_im = mag * jnp.sin(dt * a_im)
    den = a_re * a_re + a_im * a_im
    p_re = abar_re - 1.0
    f_re = (p_re * a_re + abar_im * a_im) / den
    f_im = (abar_im * a_re - p_re * a_im) / den
    b_re = b_re.astype(f32)
    b_im = b_im.astype(f32)
    bbar_re = f_re[..., None] * b_re - f_im[..., None] * b_im
    bbar_im = f_re[..., None] * b_im + f_im[..., None] * b_re
    bu_re = jnp.einsum('blgp,gnp->blgn', u32, bbar_re)
    bu_im = jnp.einsum('blgp,gnp->blgn', u32, bbar_im)
    a_seq_re = jnp.broadcast_to(abar_re, bu_re.shape)
    a_seq_im = jnp.broadcast_to(abar_im, bu_im.shape)
    _, _, s_re, s_im = lax.associative_scan(
        _ssm_combine, (a_seq_re, a_seq_im, bu_re, bu_im), axis=1)
    y = (jnp.einsum('blgn,gpn->blgp', s_re, c_re.astype(f32))
         - jnp.einsum('blgn,gpn->blgp', s_im, c_im.astype(f32)))
    y = y.reshape(bsz, seq_len, SSM_WIDTH) + d_skip.astype(f32) * u32.reshape(bsz, seq_len, SSM_WIDTH)
    y = jax.nn.gelu(y).astype(u.dtype)
    return y * jax.nn.sigmoid(y @ w_glu + b_glu)


def t5_bucket(rel):
    n = jnp.maximum(rel, 0)
    max_exact = REL_BUCKETS // 2
    nf = jnp.maximum(n, 1).astype(jnp.float32)
    large = max_exact + (jnp.log(nf / max_exact) / math.log(REL_MAX_DIST / max_exact)
                         * (REL_BUCKETS - max_exact)).astype(jnp.int32)
    large = jnp.minimum(large, REL_BUCKETS - 1)
    return jnp.where(n < max_exact, n, large)


def moba_attention(q, k, v, rel_bias):
    bsz, seq_len = q.shape[0], q.shape[1]
    n_blk = -(-seq_len // MOBA_BLOCK)
    pad = n_blk * MOBA_BLOCK - seq_len
    n_sel = min(MOBA_TOPK, n_blk - 1)
    q = q.transpose(0, 2, 1, 3)
    k = jnp.pad(k.transpose(0, 2, 1, 3), ((0, 0), (0, 0), (0, pad), (0, 0)))
    v = jnp.pad(v.transpose(0, 2, 1, 3), ((0, 0), (0, 0), (0, pad), (0, 0)))
    kb = k.reshape(bsz, ATTN_HEADS, n_blk, MOBA_BLOCK, HEAD_DIM)
    vb = v.reshape(bsz, ATTN_HEADS, n_blk, MOBA_BLOCK, HEAD_DIM)
    kmean = jnp.mean(kb.astype(jnp.float32), axis=3).astype(k.dtype)
    bias_table = rel_bias.T
    head_ix = jnp.arange(ATTN_HEADS)
    n_qblk = seq_len // Q_BLOCK
    scale = HEAD_DIM ** -0.5
    blk_off = jnp.arange(MOBA_BLOCK)

    def one_block(i):
        b = i // n_qblk
        q0 = (i % n_qblk) * Q_BLOCK
        own = q0 // MOBA_BLOCK
        qc = lax.dynamic_slice_in_dim(lax.dynamic_index_in_dim(q, b, 0, keepdims=False),
                                      q0, Q_BLOCK, axis=1)
        kb_b = lax.dynamic_index_in_dim(kb, b, 0, keepdims=False)
        vb_b = lax.dynamic_index_in_dim(vb, b, 0, keepdims=False)
        q_pos = q0 + jnp.arange(Q_BLOCK)
        k_own = lax.dynamic_index_in_dim(kb_b, own, 1, keepdims=False)
        v_own = lax.dynamic_index_in_dim(vb_b, own, 1, keepdims=False)
        rel_own = q_pos[:, None] - (own * MOBA_BLOCK + blk_off)[None, :]
        s_own = (jnp.einsum('hqd,hsd->hqs', qc, k_own).astype(jnp.float32) * scale
                 + bias_table[:, t5_bucket(rel_own)].astype(jnp.float32))
        s_own = jnp.where(rel_own[None] >= 0, s_own, NEG_INF)
        if n_sel == 0:
            p = jax.nn.softmax(s_own, axis=-1).astype(v.dtype)
            return jnp.einsum('hqs,hsd->hqd', p, v_own)
        gate = jnp.einsum('hqd,hnd->hqn', qc, kmean[b]).astype(jnp.float32)
        gate = jnp.where(jnp.arange(n_blk) < own, gate, NEG_INF)
        _, sel = lax.top_k(gate, n_sel)
        valid = jnp.arange(n_sel) < own
        kg = kb_b[head_ix[:, None, None], sel]
        vg = vb_b[head_ix[:, None, None], sel]
        rel_sel = q_pos[None, :, None, None] - (sel[..., None] * MOBA_BLOCK + blk_off)
        s_sel = (jnp.einsum('hqd,hqksd->hqks', qc, kg).astype(jnp.float32) * scale
                 + bias_table[head_ix[:, None, None, None], t5_bucket(rel_sel)].astype(jnp.float32))
        s_sel = jnp.where(valid[None, None, :, None], s_sel, NEG_INF)
        s = jnp.concatenate([s_sel.reshape(ATTN_HEADS, Q_BLOCK, n_sel * MOBA_BLOCK), s_own], axis=-1)
        p = jax.nn.softmax(s, axis=-1).astype(v.dtype)
        p_sel = p[..., :n_sel * MOBA_BLOCK].reshape(ATTN_HEADS, Q_BLOCK, n_sel, MOBA_BLOCK)
        return (jnp.einsum('hqks,hqksd->hqd', p_sel, vg)
                + jnp.einsum('hqs,hsd->hqd', p[..., n_sel * MOBA_BLOCK:], v_own))

    o = lax.map(one_block, jnp.arange(bsz * n_qblk))
    o = o.reshape(bsz, n_qblk, ATTN_HEADS, Q_BLOCK, HEAD_DIM).transpose(0, 1, 3, 2, 4)
    return o.reshape(bsz, seq_len, ATTN_WIDTH)


def setup_inputs(seed: int = 0) -> dict:
    key = jax.random.key(seed)
    ks = jax.random.split(key, 24)
    f32 = jnp.float32
    nrm = lambda k, shape, s: jax.random.normal(k, shape, f32) * s
    n_idx = jnp.arange(SSM_STATE, dtype=f32)
    a_re = -0.5 + nrm(ks[4], (DEPTH, SSM_GROUPS, SSM_STATE), 0.01)
    a_im = math.pi * n_idx + nrm(ks[5], (DEPTH, SSM_GROUPS, SSM_STATE), 0.01)
    log_dt = jax.random.uniform(ks[6], (DEPTH, SSM_GROUPS), f32,
                                math.log(DT_MIN), math.log(DT_MAX))
    return {
        "x": nrm(ks[0], (BATCH, SEQ, D_MODEL), 1.0),
        "c": nrm(ks[1], (BATCH, D_MODEL), 1.0),
        "rel_bias": nrm(ks[2], (REL_BUCKETS, ATTN_HEADS), 0.5),
        "w_ada": nrm(ks[3], (DEPTH, D_MODEL, N_MOD * D_MODEL), 0.5 * D_MODEL ** -0.5),
        "b_ada": nrm(ks[7], (DEPTH, N_MOD * D_MODEL), 0.02),
        "norm_mix_g": 1.0 + nrm(ks[8], (DEPTH, D_MODEL), 0.02),
        "w_in": nrm(ks[9], (DEPTH, D_MODEL, IN_WIDTH), D_MODEL ** -0.5),
        "ssm_a_re": a_re,
        "ssm_a_im": a_im,
        "ssm_log_dt": log_dt,
        "ssm_b_re": nrm(ks[10], (DEPTH, SSM_GROUPS, SSM_STATE, SSM_GROUP), (2 * SSM_GROUP) ** -0.5),
        "ssm_b_im": nrm(ks[11], (DEPTH, SSM_GROUPS, SSM_STATE, SSM_GROUP), (2 * SSM_GROUP) ** -0.5),
        "ssm_c_re": nrm(ks[12], (DEPTH, SSM_GROUPS, SSM_GROUP, SSM_STATE), (2 * SSM_STATE) ** -0.5),
        "ssm_c_im": nrm(ks[13], (DEPTH, SSM_GROUPS, SSM_GROUP, SSM_STATE), (2 * SSM_STATE) ** -0.5),
        "ssm_d": nrm(ks[14], (DEPTH, SSM_WIDTH), 1.0),
        "w_glu": nrm(ks[15], (DEPTH, SSM_WIDTH, SSM_WIDTH), SSM_WIDTH ** -0.5),
        "b_glu": nrm(ks[16], (DEPTH, SSM_WIDTH), 0.02),
        "w_proj_ssm": nrm(ks[17], (DEPTH, SSM_WIDTH, D_MODEL), SSM_WIDTH ** -0.5),
        "w_proj_attn": nrm(ks[18], (DEPTH, ATTN_WIDTH, D_MODEL), ATTN_WIDTH ** -0.5),
        "w_out": nrm(ks[19], (DEPTH, D_MODEL, D_MODEL), D_MODEL ** -0.5),
        "norm_mlp_g": 1.0 + nrm(ks[20], (DEPTH, D_MODEL), 0.02),
        "w_ff1": nrm(ks[21], (DEPTH, D_MODEL, D_FF), D_MODEL ** -0.5),
        "w_ff2": nrm(ks[22], (DEPTH, D_FF, D_MODEL), D_FF ** -0.5),
        "norm_final_g": 1.0 + nrm(ks[23], (D_MODEL,), 0.02),
    }


def reference(x, c, rel_bias, w_ada, b_ada, norm_mix_g, w_in, ssm_a_re, ssm_a_im, ssm_log_dt,
              ssm_b_re, ssm_b_im, ssm_c_re, ssm_c_im, ssm_d, w_glu, b_glu, w_proj_ssm,
              w_proj_attn, w_out, norm_mlp_g, w_ff1, w_ff2, norm_final_g):
    bsz, seq_len, _ = x.shape
    split_at = [SSM_WIDTH, SSM_WIDTH + ATTN_WIDTH, SSM_WIDTH + 2 * ATTN_WIDTH,
                SSM_WIDTH + 3 * ATTN_WIDTH, SSM_WIDTH + 3 * ATTN_WIDTH + D_MODEL]
    c_act = jax.nn.silu(c)
    for l in range(DEPTH):
        mod = c_act @ w_ada[l] + b_ada[l]
        sh1, sc1, g1, sh2, sc2, g2 = [m[:, None, :] for m in jnp.split(mod, N_MOD, axis=-1)]
        h = rms_norm(x, norm_mix_g[l]) * (1.0 + sc1) + sh1
        proj = h @ w_in[l]
        u, q, k, v, ga, gb = jnp.split(proj, split_at, axis=-1)
        y_ssm = s5_branch(u, ssm_a_re[l], ssm_a_im[l], ssm_log_dt[l], ssm_b_re[l], ssm_b_im[l],
                          ssm_c_re[l], ssm_c_im[l], ssm_d[l], w_glu[l], b_glu[l])
        shp = (bsz, seq_len, ATTN_HEADS, HEAD_DIM)
        y_att = moba_attention(q.reshape(shp), k.reshape(shp), v.reshape(shp), rel_bias)
        merged = (jax.nn.sigmoid(ga) * (y_ssm @ w_proj_ssm[l])
                  + jax.nn.sigmoid(gb) * (y_att @ w_proj_attn[l]))
        x = x + g1 * (merged @ w_out[l])
        h2 = rms_norm(x, norm_mlp_g[l]) * (1.0 + sc2) + sh2
        x = x + g2 * (jnp.square(jax.nn.relu(h2 @ w_ff1[l])) @ w_ff2[l])
    return rms_norm(x, norm_final_g)
```

```python
import math
from contextlib import ExitStack

import numpy as np
import concourse.bass as bass
import concourse.mybir as mybir
from concourse.bass_utils import run_bass_kernel_spmd

F32 = mybir.dt.float32
BF16 = mybir.dt.bfloat16
AF = mybir.ActivationFunctionType
ALU = mybir.AluOpType
AX = mybir.AxisListType

ENGS = ["pe", "act", "dve", "pool", "sp"]

L = 2048
D = 2048
NT = 16
DFF = 8192
EPS = 1e-6
NEG = -30000.0
TWO_PI = 2.0 * math.pi
PWS = [1, 2, 3, 4, 8, 12, 16, 32, 48, 64, 128, 192, 256, 512, 768, 1024]
LEVELS = [(1, [1, 2, 3]), (4, [1, 2, 3]), (16, [1, 2, 3]), (64, [1, 2, 3]), (256, [1, 2, 3]), (1024, [1])]


class Buf:
    __slots__ = ("name", "lw", "rd", "sem", "semval", "lastdma")

    def __init__(self, name):
        self.name = name
        self.lw = None
        self.rd = []
        self.sem = None
        self.semval = 0
        self.lastdma = None


class Op:
    __slots__ = ("eng", "fn", "deps", "signal", "ev_sem", "ev_val", "is_dma")

    def __init__(self, eng, fn, is_dma=False):
        self.eng = eng
        self.fn = fn
        self.deps = set()
        self.signal = False
        self.ev_sem = None
        self.ev_val = 0
        self.is_dma = is_dma


class Sched:
    def __init__(self, nc, stack, same_engine_sync=True):
        self.nc = nc
        self.stack = stack
        self.ops = {e: [] for e in ENGS}
        self.same = same_engine_sync
        self.esem = {e: stack.enter_context(nc.semaphore("es_" + e)) for e in ENGS}
        self.dma_bufs = []
        self.nb = 0

    def buf(self, name=None):
        self.nb += 1
        return Buf(name or ("b%d" % self.nb))

    def bufs(self, n, name="b"):
        return [self.buf("%s%d" % (name, i)) for i in range(n)]

    def _record(self, o, reads, writes):
        deps = o.deps
        for b in reads:
            if b.lw is not None:
                deps.add(b.lw)
        for b in writes:
            if b.lw is not None:
                deps.add(b.lw)
            for r in b.rd:
                if r.eng != o.eng or r.is_dma or o.is_dma:
                    deps.add(r)
        for b in reads:
            if not o.is_dma:
                b.rd = [r for r in b.rd if r.is_dma or r.eng != o.eng]
            b.rd.append(o)
        for b in writes:
            b.lw = o
            b.rd = []
        self.ops[o.eng].append(o)

    def op(self, eng, fn, reads=(), writes=()):
        o = Op(eng, fn)
        self._record(o, reads, writes)
        return o

    def dma(self, eng, out, in_, reads, writes, sembuf):
        if sembuf.sem is None:
            sembuf.sem = self.stack.enter_context(self.nc.semaphore("ds%d_%s" % (len(self.dma_bufs), sembuf.name)))
            self.dma_bufs.append(sembuf)
        o = Op(eng, lambda e: e.dma_start(out=out, in_=in_), is_dma=True)
        if sembuf.lastdma is not None:
            o.deps.add(sembuf.lastdma)
        sembuf.lastdma = o
        sembuf.semval += 16
        o.ev_sem = sembuf.sem
        o.ev_val = sembuf.semval
        o.signal = True
        self._record(o, reads, writes)
        return o

    def barrier(self):
        lasts = []
        for e in ENGS:
            for o in reversed(self.ops[e]):
                if not o.is_dma and o.fn is not None:
                    lasts.append(o)
                    break
        dmas = [b.lastdma for b in self.dma_bufs if b.lastdma is not None]
        for e in ENGS:
            o = Op(e, None)
            for l in lasts:
                if l.eng != e:
                    o.deps.add(l)
            for d in dmas:
                o.deps.add(d)
            self.ops[e].append(o)

    def _skip(self, o, d):
        if d.is_dma or o.is_dma or d.eng != o.eng:
            return False
        return d.eng == "pe" or not self.same

    def emit(self):
        for e in ENGS:
            for o in self.ops[e]:
                for d in o.deps:
                    if not d.is_dma and not self._skip(o, d):
                        d.signal = True
        for e in ENGS:
            c = 0
            for o in self.ops[e]:
                if not o.is_dma and o.signal:
                    c += 1
                    o.ev_sem = self.esem[e]
                    o.ev_val = c
            self.maxcount = getattr(self, "maxcount", {})
            self.maxcount[e] = (c, len(self.ops[e]))
        import os as _os
        if _os.environ.get("SCHED_VERBOSE"):
            print("sched counts (signals, ops):", self.maxcount, "dma sems:", {b.name: b.semval for b in self.dma_bufs})
        nc = self.nc
        engobj = {"pe": "tensor", "act": "scalar", "dve": "vector", "pool": "gpsimd", "sp": "sync"}
        with nc.Block() as block:
            for e in ENGS:
                def body(eng, ops=self.ops[e], e=e):
                    waited = {}
                    for o in ops:
                        need = {}
                        for d in o.deps:
                            if self._skip(o, d):
                                continue
                            k = id(d.ev_sem)
                            if k not in need or need[k][1] < d.ev_val:
                                need[k] = (d.ev_sem, d.ev_val)
                        for k, (s, v) in need.items():
                            if waited.get(k, 0) >= v:
                                continue
                            waited[k] = v
                            eng.wait_ge(s, v)
                        if o.fn is None:
                            continue
                        ins = o.fn(eng)
                        if o.is_dma:
                            ins.then_inc(o.ev_sem, 16)
                        elif o.signal:
                            ins.then_inc(o.ev_sem, 1)

                getattr(block, engobj[e])(body)


def apm(ap, pattern):
    return bass.AP(ap.tensor, ap.offset, [list(p) for p in pattern])


def _t5_bucket_np(rel):
    n = np.maximum(rel, 0)
    nf = np.maximum(n, 1).astype(np.float32)
    large = 16 + (np.log(nf / np.float32(16.0)) / np.float32(math.log(128 / 16)) * np.float32(16.0)).astype(np.int32)
    large = np.minimum(large, 31)
    return np.where(n < 16, n, large)


def _host_shared(inp):
    f = np.float32
    sh = {}
    sh["w_ada"] = np.ascontiguousarray(inp["w_ada"][0])
    sh["w_in"] = np.ascontiguousarray(inp["w_in"][0])
    sh["w_glu"] = np.ascontiguousarray(inp["w_glu"][0])
    sh["w_pssm"] = np.ascontiguousarray(inp["w_proj_ssm"][0])
    sh["w_patt"] = np.ascontiguousarray(inp["w_proj_attn"][0])
    sh["w_out"] = np.ascontiguousarray(inp["w_out"][0])
    sh["w_ff1"] = np.ascontiguousarray(inp["w_ff1"][0])
    sh["w_ff2"] = np.ascontiguousarray(inp["w_ff2"][0])
    sm = np.zeros((128, 512), f)
    sm[:, 0:96] = inp["b_ada"][0].reshape(96, 128).T
    sm[:, 96:112] = inp["norm_mix_g"][0].reshape(16, 128).T
    sm[:, 112:128] = inp["norm_mlp_g"][0].reshape(16, 128).T
    sm[:, 128:136] = inp["ssm_d"][0].reshape(8, 128).T
    sm[:, 136:144] = inp["b_glu"][0].reshape(8, 128).T
    sm[:, 144:152] = np.broadcast_to(inp["rel_bias"][31][None, :], (128, 8))
    sm[:64, 152] = -1.0
    sm[64:, 152] = 1.0
    sm[:64, 153] = 1.0
    sm[64:, 153] = -1.0
    for gl in range(8):
        sm[16 * gl:16 * gl + 16, 160 + gl] = 1.0
    sh["small"] = sm
    sh["gfin"] = np.ascontiguousarray(np.broadcast_to(inp["norm_final_g"][None, :], (128, D))).astype(f)
    are = inp["ssm_a_re"][0].T
    aim = inp["ssm_a_im"][0].T
    sp = np.zeros((128, 192), f)
    sp[:, 0:64] = np.concatenate([are, are], 0)
    sp[:, 64:128] = np.concatenate([aim, aim], 0)
    sp[:, 128:192] = np.broadcast_to(inp["ssm_log_dt"][0][None, :], (128, 64))
    sh["ssm_p"] = sp
    bre = inp["ssm_b_re"][0].transpose(1, 0, 2)
    bim = inp["ssm_b_im"][0].transpose(1, 0, 2)
    cre = inp["ssm_c_re"][0].transpose(2, 0, 1)
    cim = inp["ssm_c_im"][0].transpose(2, 0, 1)
    sb = np.zeros((128, 3, 64, 16), f)
    sb[:, 0] = np.concatenate([bre, bim], 0)
    sb[:, 1] = np.concatenate([bim, bre], 0)
    sb[:, 2] = np.concatenate([cre, cim], 0)
    sh["ssm_bc"] = sb.reshape(128, 3 * 64 * 16)
    cm = np.zeros((128, 3, 128), f)
    cm[:, 0] = np.eye(128, dtype=f)
    cm[:, 1] = np.roll(np.eye(128, dtype=f), 64, axis=1)
    cm[:, 2] = 1.0
    sh["cmat"] = cm.reshape(128, 384)
    rb = inp["rel_bias"]
    q = np.arange(128)[:, None]
    kk = np.arange(256)[None, :]
    bt = np.zeros((8, 128, 768), f)
    cmask = np.zeros((128, 512), f)
    for var in range(2):
        rel = 128 * var + q - kk
        bt[:, :, var * 256:(var + 1) * 256] = rb[_t5_bucket_np(rel)].transpose(2, 0, 1)
        cmask[:, var * 256:(var + 1) * 256] = np.where(rel >= 0, 0.0, NEG)
    rel = 256 + q - kk
    bt[:, :, 512:768] = rb[_t5_bucket_np(rel)].transpose(2, 0, 1)
    sh["bias_t"] = bt
    sh["cmask"] = cmask
    return sh


def _host_core(inp, b):
    return {
        "x": np.ascontiguousarray(inp["x"][b]),
        "cT": np.ascontiguousarray(inp["c"][b].reshape(16, 128).T),
    }


def build_program(dbg=False, stop_after=99, same_engine_sync=True):
    nc = bass.Bass("TRN2", target_bir_lowering=False)

    def din(name, shape, dt=F32):
        return nc.dram_tensor(name, list(shape), dt, kind="ExternalInput").ap()

    def dscr(name, shape, dt):
        if dbg:
            return nc.dram_tensor(name, list(shape), dt, kind="ExternalOutput").ap()
        return nc.dram_tensor(name, list(shape), dt).ap()

    x_d = din("x", [L, D])
    cT_d = din("cT", [128, 16])
    w_ada_d = din("w_ada", [D, 6 * D])
    w_in_d = din("w_in", [D, 8192])
    w_glu_d = din("w_glu", [1024, 1024])
    w_pssm_d = din("w_pssm", [1024, D])
    w_patt_d = din("w_patt", [1024, D])
    w_out_d = din("w_out", [D, D])
    w_ff1_d = din("w_ff1", [D, DFF])
    w_ff2_d = din("w_ff2", [DFF, D])
    small_d = din("small", [128, 512])
    gfin_d = din("gfin", [128, D])
    ssm_p_d = din("ssm_p", [128, 192])
    ssm_bc_d = din("ssm_bc", [128, 3072])
    cmat_d = din("cmat", [128, 384])
    bias_t_d = din("bias_t", [8, 128, 768])
    cmask_d = din("cmask", [128, 512])
    out_d = nc.dram_tensor("out", [L, D], F32, kind="ExternalOutput").ap()

    uT_d = dscr("uT_s", [1024, L], BF16)
    qT_d = dscr("qT_s", [1024, L], BF16)
    kT_d = dscr("kT_s", [1024, L], BF16)
    v_d = dscr("v_s", [L, 1024], BF16)
    gaT_d = dscr("gaT_s", [D, L], BF16)
    gbT_d = dscr("gbT_s", [D, L], BF16)
    mssm_d = dscr("mssm_s", [D, L], BF16)
    x1_d = dscr("x1_s", [L, D], F32)
    if dbg:
        dbg_modT = nc.dram_tensor("dbg_modT", [128, 96], F32, kind="ExternalOutput").ap()
        dbg_yssm = nc.dram_tensor("dbg_yssm", [1024, L], BF16, kind="ExternalOutput").ap()
        dbg_yatt = nc.dram_tensor("dbg_yatt", [L, 1024], BF16, kind="ExternalOutput").ap()

    with ExitStack() as st:
        S = Sched(nc, st, same_engine_sync)
        ent = st.enter_context

        small = ent(nc.sbuf_tensor("small_sb", [128, 512], F32))
        cmat = ent(nc.sbuf_tensor("cmat_sb", [128, 384], F32))
        ident = ent(nc.sbuf_tensor("ident", [128, 128], BF16))
        misc = ent(nc.sbuf_tensor("misc", [128, 512], F32))
        gbc = ent(nc.sbuf_tensor("gbc", [128, 2, D], F32))
        wsl = [ent(nc.sbuf_tensor("wsl%d" % i, [128, 8192], BF16)) for i in range(3)]
        NA = 136000
        arena = ent(nc.sbuf_tensor("arena", [128, NA // 2], BF16))
        PS = [ent(nc.psum_tensor("ps%d" % i, [128, 512], F32)) for i in range(8)]
        b_PS = S.bufs(8, "ps")
        b_wsl = S.bufs(3, "wsl")
        b_small = S.buf("small")
        b_cmat = S.buf("cmat")
        b_ident = S.buf("ident")
        b_gbc = S.bufs(2, "gbc")

        identf = cmat[:, 0:128]
        swapf = cmat[:, 128:256]
        onesf = cmat[:, 256:384]

        def carve(off, shape, dt):
            n = int(np.prod(shape))
            esz = 2 if dt == BF16 else 4
            assert off % 4 == 0 and off + n * esz <= NA, (off, shape, NA)
            ap = arena[:, off // 2: off // 2 + n * esz // 2]
            if dt != BF16:
                ap = ap.bitcast(dt)
            if len(shape) == 2:
                ap = ap.rearrange("p (a b) -> p a b", b=shape[1])
            elif len(shape) == 3:
                ap = ap.rearrange("p (a b c) -> p a b c", b=shape[1], c=shape[2])
            return ap

        def mm(out, lhsT, rhs, start, stop, reads, writes):
            S.op("pe", lambda e: e.matmul(out, lhsT=lhsT, rhs=rhs, start=start, stop=stop), reads, writes)

        def tr(out, in_, idn, reads, writes):
            S.op("pe", lambda e: e.transpose(out=out, in_=in_, identity=idn), reads, writes)

        def act(out, in_, func, reads, writes, bias=None, scale=None, accum=None):
            kw = {}
            if bias is not None:
                kw["bias"] = bias
            if scale is not None:
                kw["scale"] = scale
            if accum is not None:
                kw["accum_out"] = accum
            S.op("act", lambda e: e.activation(out=out, in_=in_, func=func, **kw), reads, writes)

        def ts(eng, out, in0, s1, s2, op0, op1, reads, writes):
            if op1 is None:
                S.op(eng, lambda e: e.tensor_scalar(out=out, in0=in0, scalar1=s1, scalar2=None, op0=op0), reads, writes)
            else:
                S.op(eng, lambda e: e.tensor_scalar(out=out, in0=in0, scalar1=s1, scalar2=s2, op0=op0, op1=op1), reads, writes)

        def tt(eng, out, in0, in1, op, reads, writes):
            S.op(eng, lambda e: e.tensor_tensor(out=out, in0=in0, in1=in1, op=op), reads, writes)

        def stt(out, in0, scalar, in1, op0, op1, reads, writes):
            S.op("dve", lambda e: e.scalar_tensor_tensor(out=out, in0=in0, scalar=scalar, in1=in1, op0=op0, op1=op1), reads, writes)

        def cp(eng, out, in_, reads, writes):
            if eng == "act":
                act(out, in_, AF.Copy, reads, writes)
            else:
                S.op(eng, lambda e: e.tensor_copy(out=out, in_=in_), reads, writes)

        evc = [0]

        def evac(out, in_, reads, writes):
            evc[0] += 1
            cp("act" if evc[0] % 2 else "dve", out, in_, reads, writes)

        psr = [0]

        def next_ps(lo=0, hi=8):
            psr[0] += 1
            i = lo + psr[0] % (hi - lo)
            return PS[i], b_PS[i]

        slabs = []
        wstate = {"issued": 0, "used": 0}

        def wview(w, k0, nk, c0, ncol):
            return w.rearrange("(k p) c -> p k c", p=128)[:, k0:k0 + nk, c0:c0 + ncol]

        def plan_slabs():
            for s in range(8):
                slabs.append((wview(w_ada_d, 0, 16, s * 512, 512), 16, 512))
            for s in range(16):
                slabs.append((wview(w_in_d, 0, 16, s * 512, 512), 16, 512))
                slabs.append((wview(w_ada_d, 0, 16, (8 + s) * 512, 512), 16, 512))
            for s in range(2):
                slabs.append((wview(w_glu_d, 0, 8, s * 512, 512), 8, 512))
            for s in range(4):
                slabs.append((wview(w_pssm_d, 0, 8, s * 512, 512), 8, 512))
            for s in range(4):
                slabs.append((wview(w_patt_d, 0, 8, s * 512, 512), 8, 512))
            for s in range(4):
                slabs.append((wview(w_out_d, 0, 16, s * 512, 512), 16, 512))
            for tc in range(4):
                for s in range(16):
                    slabs.append((wview(w_ff1_d, 0, 16, s * 512, 512), 16, 512))
                for cs in range(4):
                    for kg in range(4):
                        slabs.append((wview(w_ff2_d, kg * 16, 16, cs * 512, 512), 16, 512))

        plan_slabs()

        import os as _os2
        _wcap = int(_os2.environ.get("WCAP", "100000"))

        def wissue_upto(n):
            while wstate["issued"] < min(n, len(slabs), _wcap):
                i = wstate["issued"]
                view, nk, ncol = slabs[i]
                sl = wsl[i % 3][:, 0:nk * ncol].rearrange("p (k c) -> p k c", c=ncol)
                S.dma("pool", sl, view, [], [b_wsl[i % 3]], b_wsl[i % 3])
                wstate["issued"] += 1

        def wget():
            i = wstate["used"]
            wissue_upto(i + 3)
            wstate["used"] += 1
            view, nk, ncol = slabs[i]
            sl = wsl[i % 3][:, 0:nk * ncol].rearrange("p (k c) -> p k c", c=ncol)
            return sl, b_wsl[i % 3]

        S.dma("sp", small[:], small_d, [], [b_small], b_small)
        S.dma("sp", cmat[:], cmat_d, [], [b_cmat], b_cmat)
        cp("dve", ident[:], identf, [b_cmat], [b_ident])
        b_misc = S.buf("misc")
        b_modT = S.buf("modT")
        cT_sb = misc[:, 0:16]
        cact = misc[:, 16:24].bitcast(BF16)
        modT = misc[:, 32:128]
        A1 = misc[:, 128:144]
        A2 = misc[:, 144:160]
        S.dma("sp", cT_sb, cT_d, [], [b_misc], b_misc)
        wissue_upto(3)

        act(cact, cT_sb, AF.Silu, [b_misc], [b_misc])
        modps, b_modps = PS[7], b_PS[7]
        def mod_slab(s):
            wslab, wb = wget()
            for jj in range(4):
                j = s * 4 + jj
                for kt in range(16):
                    mm(modps[:, j:j + 1], wslab[:, kt, jj * 128:(jj + 1) * 128], cact[:, kt:kt + 1],
                       kt == 0, kt == 15, [wb, b_misc], [b_modps])

        for s in range(8):
            mod_slab(s)
        b_modT2 = S.buf("modT2")
        tt("dve", modT[:, 0:32], modps[:, 0:32], small[:, 0:32], ALU.add, [b_modps, b_small], [b_modT])
        stt(A1, modT[:, 16:32], 1.0, small[:, 96:112], ALU.add, ALU.mult, [b_modT, b_small], [b_modT])
        sh1 = modT[:, 0:16]
        sh2 = modT[:, 48:64]

        def mod_finish():
            tt("dve", modT[:, 32:96], modps[:, 32:96], small[:, 32:96], ALU.add, [b_modps, b_small], [b_modT2])
            stt(A2, modT[:, 64:80], 1.0, small[:, 112:128], ALU.add, ALU.mult, [b_modT2, b_small], [b_modT2])
            if dbg:
                S.dma("sp", dbg_modT, modT, [b_modT, b_modT2], [], b_modT2)
            bcast_gates()
        Gt = [ent(nc.sbuf_tensor("Gt%d" % i, [128, 128], F32)) for i in range(2)]
        b_Gt = S.bufs(2, "Gt")

        def bcast_gates():
            for which, c0 in ((0, 32), (1, 80)):
                for grp in range(4):
                    ps, bps = next_ps(0, 6)
                    for jj in range(4):
                        t = grp * 4 + jj
                        gi = t % 2
                        ts("dve", Gt[gi][:], onesf, modT[:, c0 + t:c0 + t + 1], None, ALU.mult, None, [b_modT2, b_cmat], [b_Gt[gi]])
                        mm(ps[:, jj * 128:(jj + 1) * 128], Gt[gi][:], identf, True, True, [b_Gt[gi], b_cmat], [bps])
                    evac(gbc[:, which, grp * 512:(grp + 1) * 512], ps[:, :], [bps], [b_gbc[which]])

        if stop_after <= 0:
            for s in range(16):
                wget()
                mod_slab(8 + s)
            mod_finish()
            S.barrier()
            S.emit()
            return nc

        hT = carve(0, [16, L], BF16)
        b_hT = S.bufs(16, "hT")
        o = 65536
        xt = [carve(o, [D], F32), carve(o + 8192, [D], F32)]
        b_xt = S.bufs(2, "xt")
        o += 16384
        junk = carve(o, [D], BF16)
        b_junk = S.buf("junk")
        o += 4096
        xn = [carve(o, [D], BF16), carve(o + 4096, [D], BF16)]
        b_xn = S.bufs(2, "xn")
        o += 8192
        stg = [carve(o + i * 4096, [L], BF16) for i in range(3)]
        b_stg = [S.bufs(4, "stg%d_" % i) for i in range(3)]
        o += 12288
        stat = misc[:, 160:224]
        b_stat = S.bufs(16, "stat")

        def norm_tile(src, bsrc, tt_i, xn_ap, b_xn_i, statcol, bst):
            act(junk, src, AF.Square, [bsrc], [b_junk, bst], accum=statcol[:, 0:1])
            ts("dve", statcol[:, 1:2], statcol[:, 0:1], 1.0 / D, EPS, ALU.mult, ALU.add, [bst], [bst])
            act(statcol[:, 2:3], statcol[:, 1:2], AF.Sqrt, [bst], [bst])
            S.op("dve", lambda e: e.reciprocal(out=statcol[:, 3:4], in_=statcol[:, 2:3]), [bst], [bst])
            ts("dve", xn_ap, src, statcol[:, 3:4], None, ALU.mult, None, [bsrc, bst], [b_xn_i])

        def transpose_mod(xn_ap, b_xn_i, dst, b_dst, tokslice, Acol, shcol, b_mod):
            for fg in range(4):
                ps, bps = next_ps(0, 7)
                psb = ps[:, :].bitcast(BF16)
                for j in range(4):
                    ft = fg * 4 + j
                    tr(psb[:, j * 128:(j + 1) * 128], xn_ap[:, ft * 128:(ft + 1) * 128], ident[:], [b_xn_i, b_ident], [bps])
                for j in range(4):
                    ft = fg * 4 + j
                    if fg % 2 == 0:
                        ts("dve", dst[:, ft, tokslice], psb[:, j * 128:(j + 1) * 128], Acol[:, ft:ft + 1], shcol[:, ft:ft + 1],
                           ALU.mult, ALU.add, [bps, b_mod], [b_dst[ft]])
                    else:
                        act(dst[:, ft, tokslice], psb[:, j * 128:(j + 1) * 128], AF.Identity, [bps, b_mod], [b_dst[ft]],
                            bias=shcol[:, ft:ft + 1], scale=Acol[:, ft:ft + 1])

        S.dma("sp", xt[0], x_d[0:128, :], [], [b_xt[0]], b_xt[0])
        for ti in range(NT):
            if ti + 1 < NT:
                S.dma("sp", xt[(ti + 1) % 2], x_d[(ti + 1) * 128:(ti + 2) * 128, :], [], [b_xt[(ti + 1) % 2]], b_xt[(ti + 1) % 2])
            sc = stat[:, 4 * ti:4 * ti + 4]
            norm_tile(xt[ti % 2], b_xt[ti % 2], ti, xn[ti % 2], b_xn[ti % 2], sc, b_stat[ti])
            transpose_mod(xn[ti % 2], b_xn[ti % 2], hT, b_hT, slice(ti * 128, (ti + 1) * 128), A1, sh1, b_modT)

        if stop_after <= 1:
            if dbg:
                for kt in range(16):
                    S.dma("sp", gaT_d[kt * 128:(kt + 1) * 128, :], hT[:, kt, :], [b_hT[kt]], [], b_hT[kt])
            S.barrier()
            S.emit()
            return nc
        QSCALE = 128.0 ** -0.5
        sidx = [0]
        p2n = [0]

        def form2_slab(dst_d, row0, post):
            wslab, wb = wget()
            for ct in range(4):
                si = sidx[0] % 3
                sidx[0] += 1
                for c in range(4):
                    ps, bps = next_ps(0, 7)
                    for kt in range(16):
                        mm(ps[:, :], wslab[:, kt, ct * 128:(ct + 1) * 128], hT[:, kt, c * 512:(c + 1) * 512],
                           kt == 0, kt == 15, [wb, b_hT[kt]], [bps])
                    dst = stg[si][:, c * 512:(c + 1) * 512]
                    if post == "sig":
                        act(dst, ps[:, :], AF.Sigmoid, [bps], [b_stg[si][c]])
                    elif post == "q":
                        ts("dve", dst, ps[:, :], QSCALE, None, ALU.mult, None, [bps], [b_stg[si][c]])
                    else:
                        evac(dst, ps[:, :], [bps], [b_stg[si][c]])
                r = row0 + ct * 128
                S.dma("sp", dst_d[r:r + 128, :], stg[si], b_stg[si], [], b_stg[si][0])
            mod_slab(8 + p2n[0])
            p2n[0] += 1

        def form1_slab(dst_d, col0):
            wslab, wb = wget()
            for ti in range(NT):
                si = sidx[0] % 3
                sidx[0] += 1
                ps, bps = next_ps(0, 7)
                for kt in range(16):
                    mm(ps[:, :], hT[:, kt, ti * 128:(ti + 1) * 128], wslab[:, kt, :], kt == 0, kt == 15, [wb, b_hT[kt]], [bps])
                evac(stg[si][:, 0:512], ps[:, :], [bps], [b_stg[si][0]])
                S.dma("sp", dst_d[ti * 128:(ti + 1) * 128, col0:col0 + 512], stg[si][:, 0:512], [b_stg[si][0]], [], b_stg[si][0])
            mod_slab(8 + p2n[0])
            p2n[0] += 1

        for s in range(2):
            form2_slab(uT_d, s * 512, "copy")
        for s in range(2):
            form2_slab(qT_d, s * 512, "q")
        for s in range(2):
            form2_slab(kT_d, s * 512, "copy")
        for s in range(2):
            form1_slab(v_d, s * 512)
        for s in range(4):
            form2_slab(gaT_d, s * 512, "sig")
        for s in range(4):
            form2_slab(gbT_d, s * 512, "sig")
        mod_finish()
        S.barrier()
        if stop_after <= 2:
            S.barrier()
            S.emit()
            return nc

        o = 0
        sp_ = carve(o, [192], F32); o += 768
        b_sp = S.buf("ssm_p")
        S.dma("sp", sp_, ssm_p_d, [], [b_sp], b_sp)
        bc = carve(o, [3, 64, 16], F32); o += 12288
        b_bc = S.buf("ssm_bc")
        S.dma("sp", bc.rearrange("p a b c -> p (a b c)"), ssm_bc_d, [], [b_bc], b_bc)
        NTMP = 28
        tmp = [carve(o + i * 256, [64], F32) for i in range(NTMP)]
        o += NTMP * 256
        b_t = S.buf("ssmtmp")
        T1 = carve(o, [64, 16], F32); o += 4096
        T2 = carve(o, [64, 16], F32); o += 4096
        Cs = carve(o, [64, 16], BF16); o += 2048
        o_setup_end = max(o, 32768)
        are, aim, ldt = sp_[:, 0:64], sp_[:, 64:128], sp_[:, 128:192]
        (dt_, th, lm, mag, phs, phc, tq, sn, cs, ar, ai, pr, den, rden, fre, fim, fims, t1, t2, t3, t4, thc) = tmp[:22]
        R_ = [b_t, b_sp]
        W_ = [b_t]
        act(dt_, ldt, AF.Exp, [b_sp], W_)
        tt("dve", th, dt_, aim, ALU.mult, R_, W_)
        tt("dve", lm, dt_, are, ALU.mult, R_, W_)
        act(mag, lm, AF.Exp, R_, W_)
        ts("dve", thc, th, math.pi / 2, None, ALU.add, None, R_, W_)
        for src, dst in ((th, phs), (thc, phc)):
            cp("dve", dst, src, R_, W_)
            for m in range(5):
                thr = (2 * m + 1) * math.pi
                ts("dve", tq, src, thr, -TWO_PI, ALU.is_gt, ALU.mult, R_, W_)
                tt("dve", dst, dst, tq, ALU.add, R_, W_)
        act(sn, phs, AF.Sin, R_, W_)
        act(cs, phc, AF.Sin, R_, W_)
        tt("dve", ar, mag, cs, ALU.mult, R_, W_)
        tt("dve", ai, mag, sn, ALU.mult, R_, W_)
        ts("dve", pr, ar, -1.0, None, ALU.add, None, R_, W_)
        tt("dve", t1, are, are, ALU.mult, R_, W_)
        tt("dve", t2, aim, aim, ALU.mult, R_, W_)
        tt("dve", den, t1, t2, ALU.add, R_, W_)
        S.op("dve", lambda e: e.reciprocal(out=rden, in_=den), R_, W_)
        tt("dve", t1, pr, are, ALU.mult, R_, W_)
        tt("dve", t2, ai, aim, ALU.mult, R_, W_)
        tt("dve", t3, t1, t2, ALU.add, R_, W_)
        tt("dve", fre, t3, rden, ALU.mult, R_, W_)
        tt("dve", t1, ai, are, ALU.mult, R_, W_)
        tt("dve", t2, pr, aim, ALU.mult, R_, W_)
        tt("dve", t3, t1, t2, ALU.subtract, R_, W_)
        tt("dve", fim, t3, rden, ALU.mult, R_, W_)
        ts("dve", fims, fim, small[:, 152:153], None, ALU.mult, None, R_ + [b_small], W_)

        def bc16(a):
            return apm(a, [a.ap[0], [1, 64], [0, 16]])

        b_T = S.buf("T12")
        tt("dve", T1, bc[:, 0], bc16(fre), ALU.mult, [b_bc, b_t], [b_T])
        tt("dve", T2, bc[:, 1], bc16(fims), ALU.mult, [b_bc, b_t], [b_T])
        tt("dve", T1, T1, T2, ALU.add, [b_T], [b_T])
        ts("dve", Cs, bc[:, 2], small[:, 153:154], None, ALU.mult, None, [b_bc, b_small], [b_T])
        o = o_setup_end
        BU = carve(o, [64, 128], BF16); o += 16384
        CPAD = carve(o, [64, 128], BF16); o += 16384
        pwre = carve(o, [16, 64], F32); o += 4096
        pwim = carve(o, [16, 64], F32); o += 4096
        b_BU = S.buf("BU")
        b_CP = S.buf("CPAD")
        b_pw = S.buf("pw")
        for t8 in range(8):
            ps, bps = next_ps(0, 8)
            src = T1[:, t8 * 8:(t8 + 1) * 8, :].rearrange("p a b -> p (a b)")
            tr(ps[:, 0:128], src, identf, [b_T, b_cmat], [bps])
            tb = carve(o, [128], F32)
            b_tb = b_T
            cp("act", tb, ps[:, 0:128], [bps], [b_tb])
            for gl in range(8):
                ts("dve" if gl % 2 else "pool", BU[:, t8 * 8 + gl, :], tb, small[:, 160 + gl:161 + gl], None, ALU.mult, None,
                   [b_tb, b_small], [b_BU])
        S.op("pool", lambda e: e.memset(CPAD.rearrange("p a b -> p (a b)"), 0.0), [], [b_CP])
        CP4 = CPAD.rearrange("p (t g) c -> p t g c", g=8)
        Cs4 = Cs.rearrange("p (t g) c -> p t g c", g=8)
        for gl in range(8):
            cp("pool", CP4[:, :, gl, 16 * gl:16 * gl + 16], Cs4[:, :, gl, :], [b_T], [b_CP])
        cp("dve", pwre[:, 0, :], ar, R_, [b_pw])
        cp("dve", pwim[:, 0, :], ai, R_, [b_pw])

        def cmul(k, a, b):
            tt("dve", t1, pwre[:, a, :], pwre[:, b, :], ALU.mult, [b_pw, b_t], W_)
            tt("dve", t2, pwim[:, a, :], pwim[:, b, :], ALU.mult, [b_pw, b_t], W_)
            tt("dve", pwre[:, k, :], t1, t2, ALU.subtract, [b_t], [b_pw])
            tt("dve", t3, pwre[:, a, :], pwim[:, b, :], ALU.mult, [b_pw, b_t], W_)
            tt("dve", t4, pwim[:, a, :], pwre[:, b, :], ALU.mult, [b_pw, b_t], W_)
            tt("dve", pwim[:, k, :], t3, t4, ALU.add, [b_t], [b_pw])

        for k in range(1, 16):
            p = PWS[k]
            best = None
            for a in range(k):
                for b in range(a, k):
                    if PWS[a] + PWS[b] == p:
                        best = (a, b)
            cmul(k, best[0], best[1])
        ts("dve", pwim.rearrange("p a b -> p (a b)"), pwim.rearrange("p a b -> p (a b)"), small[:, 153:154], None, ALU.mult, None,
           [b_pw, b_small], [b_pw])
        S.barrier()

        o += 512
        uTs = [carve(o, [L], BF16), carve(o + 4096, [L], BF16)]; o += 8192
        b_uT = S.bufs(2, "uT")
        AM = [carve(o + i * 4096, [16, 128], BF16) for i in range(3)]; o += 12288
        b_AM = S.bufs(3, "AM")
        tAs = [carve(o, [16, 128], BF16)]; o += 4096
        tBs = [carve(o, [16, 128], BF16)]; o += 4096
        b_tA = S.bufs(1, "tA")
        b_tB = S.bufs(1, "tB")
        Sb = [carve(o + i * 4096, [L], BF16) for i in range(6)]; o += 24576
        b_Sb = [S.bufs(4, "Sb%d_" % i) for i in range(6)]
        vtmp = [carve(o + i * 1024, [512], BF16) for i in range(4)]; o += 4096
        b_vt = S.bufs(4, "vtmp")
        ytmp = [carve(o + i * 2048, [512], F32) for i in range(1)]; o += 2048
        b_yt = S.bufs(1, "ytmp")
        o_scan_end = o
        yT = carve(0, [8, L], BF16)
        assert 32768 <= o_setup_end
        b_yT = S.buf("yT")
        dT = small[:, 128:136]
        bglu = small[:, 136:144]

        def bcI(m):
            return apm(m, [m.ap[0], [0, 16], [1, 128]])

        def bcP(pw, g):
            a = pw[:, :, g:g + 1]
            return apm(a, [a.ap[0], [64, 16], [0, 128]])

        YP = [4, 5, 6, 7]
        pcnt = [0]

        def group_gen(g, par):
            t8, gl = g // 8, g % 8
            if gl == 0:
                S.dma("sp", uTs[t8 % 2], uT_d[t8 * 128:(t8 + 1) * 128, :], [], [b_uT[t8 % 2]], b_uT[t8 % 2])
            u_sb, b_u = uTs[t8 % 2], b_uT[t8 % 2]
            am, b_am = AM[par], b_AM[par]
            tt("dve", tAs[0], bcI(identf), bcP(pwre, g), ALU.mult, [b_cmat, b_pw], [b_tA[0]])
            tt("pool", tBs[0], bcI(swapf), bcP(pwim, g), ALU.mult, [b_cmat, b_pw], [b_tB[0]])
            tt("dve", am, tAs[0], tBs[0], ALU.add, [b_tA[0], b_tB[0]], [b_am])
            yield
            cur = par * 2
            nxt = par * 2 + 1
            for c in range(4):
                ps, bps = next_ps(0, 4)
                mm(ps[:, :], BU[:, g, :], u_sb[:, c * 512:(c + 1) * 512], True, True, [b_BU, b_u], [bps])
                evac(Sb[cur][:, c * 512:(c + 1) * 512], ps[:, :], [bps], [b_Sb[cur][c]])
                yield
            for (dd, ms) in LEVELS:
                for c in range(4):
                    c0 = c * 512
                    terms = []
                    for m in ms:
                        shf = m * dd
                        if shf >= c0 + 512:
                            continue
                        lo = max(0, shf - c0)
                        n = 512 - lo
                        s0 = c0 + lo - shf
                        rb_ = [b_Sb[cur][k] for k in range(s0 // 512, (s0 + n - 1) // 512 + 1)]
                        terms.append((am[:, PWS.index(shf), :], lo, s0, n, rb_))
                    pcnt[0] += 1
                    dst = Sb[nxt][:, c0:c0 + 512]
                    if not terms:
                        cp("pool", dst, Sb[cur][:, c0:c0 + 512], [b_Sb[cur][c]], [b_Sb[nxt][c]])
                    elif pcnt[0] % 2 == 0:
                        ps, bps = next_ps(0, 4)
                        mm(ps[:, :], ident[:], Sb[cur][:, c0:c0 + 512], True, False, [b_ident, b_Sb[cur][c]], [bps])
                        for i, (lh, lo, s0, n, rb_) in enumerate(terms):
                            mm(ps[:, lo:lo + n], lh, Sb[cur][:, s0:s0 + n], False, i == len(terms) - 1, [b_am] + rb_, [bps])
                        cp("act", dst, ps[:, :], [bps], [b_Sb[nxt][c]])
                    else:
                        ps, bps = next_ps(0, 4)
                        for i, (lh, lo, s0, n, rb_) in enumerate(terms):
                            mm(ps[:, lo:lo + n], lh, Sb[cur][:, s0:s0 + n], i == 0, i == len(terms) - 1, [b_am] + rb_, [bps])
                        lo0 = terms[0][1]
                        if lo0 > 0:
                            cp("pool", Sb[nxt][:, c0:c0 + lo0], Sb[cur][:, c0:c0 + lo0], [b_Sb[cur][c]], [b_Sb[nxt][c]])
                        tt("dve", Sb[nxt][:, c0 + lo0:c0 + 512], ps[:, lo0:512], Sb[cur][:, c0 + lo0:c0 + 512], ALU.add,
                           [bps, b_Sb[cur][c]], [b_Sb[nxt][c]])
                    yield
                cur, nxt = nxt, cur
            for c in range(4):
                mm(PS[YP[c]][:, :], CPAD[:, g, :], Sb[cur][:, c * 512:(c + 1) * 512], gl == 0, gl == 7,
                   [b_CP, b_Sb[cur][c]], [b_PS[YP[c]]])
            if gl == 7:
                for c in range(4):
                    cs_ = slice(c * 512, (c + 1) * 512)
                    stt(vtmp[c], u_sb[:, cs_], dT[:, t8:t8 + 1], PS[YP[c]][:, :], ALU.mult, ALU.add, [b_u, b_small, b_PS[YP[c]]], [b_vt[c]])
            yield
            if gl == 7:
                for c in range(4):
                    cs_ = slice(c * 512, (c + 1) * 512)
                    v_, bv = vtmp[c], b_vt[c]
                    w_, bw = ytmp[0], b_yt[0]
                    act(w_, v_, AF.Square, [bv], [bw])
                    ts("dve", w_, w_, 0.044715, 1.0, ALU.mult, ALU.add, [bw], [bw])
                    tt("pool", w_, w_, v_, ALU.mult, [bw, bv], [bw])
                    act(w_, w_, AF.Sigmoid, [bw], [bw], scale=1.5957691216057308)
                    tt("dve", yT[:, t8, cs_], v_, w_, ALU.mult, [bv, bw], [b_yT])
                    yield

        pending = list(range(64))
        active = []
        free_slots = [0, 1, 2]
        while pending or active:
            while pending and free_slots:
                g = pending.pop(0)
                slot = free_slots.pop(0)
                active.append((group_gen(g, slot), slot))
            for item in list(active):
                try:
                    next(item[0])
                except StopIteration:
                    active.remove(item)
                    free_slots.append(item[1])
        S.barrier()
        if dbg:
            for t8 in range(8):
                S.dma("sp", dbg_yssm[t8 * 128:(t8 + 1) * 128, :], yT[:, t8, :], [b_yT], [], b_yT)

        o = 32768
        ygT = carve(o, [8, L], BF16); o += 32768
        b_yg = S.buf("ygT")
        gsb = [carve(o, [L], BF16), carve(o + 4096, [L], BF16)]; o += 8192
        b_gsb = S.bufs(2, "gsb")
        sg = [carve(o, [512], BF16), carve(o + 1024, [512], BF16)]; o += 2048
        b_sg = S.bufs(2, "sg")
        mst = [carve(o, [L], BF16), carve(o + 4096, [L], BF16)]; o += 8192
        b_mst = S.bufs(2, "mst")
        n_ = 0
        for ct in range(8):
            if ct % 4 == 0:
                wslab, wb = wget()
            for c in range(4):
                cs_ = slice(c * 512, (c + 1) * 512)
                ps, bps = next_ps(0, 8)
                for kt in range(8):
                    mm(ps[:, :], wslab[:, kt, (ct % 4) * 128:(ct % 4 + 1) * 128], yT[:, kt, cs_], kt == 0, kt == 7, [wb, b_yT], [bps])
                i2 = n_ % 2
                n_ += 1
                act(sg[i2], ps[:, :], AF.Sigmoid, [bps, b_small], [b_sg[i2]], bias=bglu[:, ct:ct + 1])
                tt("dve", ygT[:, ct, cs_], sg[i2], yT[:, ct, cs_], ALU.mult, [b_sg[i2], b_yT], [b_yg])
        for half in range(4):
            wslab, wb = wget()
            for fl in range(4):
                ft = half * 4 + fl
                i2 = ft % 2
                S.dma("sp", gsb[i2], gaT_d[ft * 128:(ft + 1) * 128, :], [], [b_gsb[i2]], b_gsb[i2])
                for c in range(4):
                    cs_ = slice(c * 512, (c + 1) * 512)
                    ps, bps = next_ps(0, 8)
                    for kt in range(8):
                        mm(ps[:, :], wslab[:, kt, fl * 128:(fl + 1) * 128], ygT[:, kt, cs_], kt == 0, kt == 7, [wb, b_yg], [bps])
                    tt("dve", mst[i2][:, cs_], ps[:, :], gsb[i2][:, cs_], ALU.mult, [bps, b_gsb[i2]], [b_mst[i2]])
                S.dma("sp", mssm_d[ft * 128:(ft + 1) * 128, :], mst[i2], [b_mst[i2]], [], b_mst[i2])
        S.barrier()
        if stop_after <= 3:
            S.barrier()
            S.emit()
            return nc

        yatt = carve(98304, [NT, 1024], BF16)
        b_yatt = S.buf("yatt")
        o = 0
        qh = [carve(o, [L], BF16), carve(o + 4096, [L], BF16)]; o += 8192
        kh = [carve(o, [L], BF16), carve(o + 4096, [L], BF16)]; o += 8192
        vh = [carve(o, [NT, 128], BF16), carve(o + 4096, [NT, 128], BF16)]; o += 8192
        bh = [carve(o, [768], F32), carve(o + 3072, [768], F32)]; o += 6144
        b_qh = S.bufs(2, "qh"); b_kh = S.bufs(2, "kh"); b_vh = S.bufs(2, "vh"); b_bh = S.bufs(2, "bh")
        cmk = carve(o, [512], F32); o += 2048
        b_cmk = S.buf("cmk")
        S.dma("sp", cmk, cmask_d, [], [b_cmk], b_cmk)
        Ssb = [carve(o, [L], F32), carve(o + 8192, [L], F32)]; o += 16384
        b_Ssb = [S.bufs(8, "Ssb%d_" % i) for i in range(2)]
        Pb = [carve(o, [L], BF16), carve(o + 4096, [L], BF16)]; o += 8192
        b_Pb = S.bufs(2, "Pb")
        PT = [carve(o, [16, 128], BF16), carve(o + 4096, [16, 128], BF16)]; o += 8192
        b_PT = [S.bufs(2, "PT%d_" % i) for i in range(2)]
        kms = carve(o, [8], F32); o += 32
        kmb = carve(o, [8], BF16); o += 32
        b_km = S.buf("km")
        NSM = 8
        smv = [carve(o + i * 128, [32], F32) for i in range(NSM)]; o += NSM * 128
        b_smv = S.bufs(NSM, "smv")
        gsbuf = carve(o, [8], F32); o += 32
        b_gs = S.buf("gatesb")
        c31 = small[:, 144:152]
        vT_view = v_d.rearrange("(t p) c -> p t c", p=128)

        def load_head(h):
            i2 = h % 2
            S.dma("sp", qh[i2], qT_d[h * 128:(h + 1) * 128, :], [], [b_qh[i2]], b_qh[i2])
            S.dma("sp", kh[i2], kT_d[h * 128:(h + 1) * 128, :], [], [b_kh[i2]], b_kh[i2])
            S.dma("sp", vh[i2], vT_view[:, :, h * 128:(h + 1) * 128], [], [b_vh[i2]], b_vh[i2])
            S.dma("sp", bh[i2], bias_t_d[h], [], [b_bh[i2]], b_bh[i2])
            tt("pool", bh[i2][:, 0:512], bh[i2][:, 0:512], cmk, ALU.add, [b_bh[i2], b_cmk], [b_bh[i2]])

        kms2 = [kms, carve(o, [8], F32)]; o += 32
        kmb2 = [kmb, carve(o, [8], BF16)]; o += 32
        gs2 = [gsbuf, carve(o, [8], F32)]; o += 32
        b_km2 = [b_km, S.buf("km1")]
        b_gs2 = [b_gs, S.buf("gs1")]

        def stream_gen(st_):
            sb0 = 4 * st_
            pt_i = sb0 + 2
            ob = sb0 + 3
            kms_, kmb_, gsb_, bkm, bgs = kms2[st_], kmb2[st_], gs2[st_], b_km2[st_], b_gs2[st_]
            q_, k_, v_, bt_ = qh[st_], kh[st_], vh[st_], bh[st_]
            bq, bk, bv, bb = b_qh[st_], b_kh[st_], b_vh[st_], b_bh[st_]
            Sd, bS = Ssb[st_], b_Ssb[st_]
            Pd, bP = Pb[st_], b_Pb[st_]
            PTd, bPT = PT[st_], b_PT[st_]
            n = 0
            for h in range(4 * st_, 4 * st_ + 4):
                S.dma("sp", q_, qT_d[h * 128:(h + 1) * 128, :], [], [bq], bq)
                S.dma("sp", k_, kT_d[h * 128:(h + 1) * 128, :], [], [bk], bk)
                S.dma("sp", v_, vT_view[:, :, h * 128:(h + 1) * 128], [], [bv], bv)
                S.dma("sp", bt_, bias_t_d[h], [], [bb], bb)
                tt("pool", bt_[:, 0:512], bt_[:, 0:512], cmk, ALU.add, [bb, b_cmk], [bb])
                S.op("dve", lambda e, k_=k_, kms_=kms_: e.tensor_reduce(out=kms_, in_=k_.rearrange("p (a b) -> p a b", b=256), axis=AX.X, op=ALU.add),
                     [bk], [bkm])
                ts("dve", kmb_, kms_, 1.0 / 256.0, None, ALU.mult, None, [bkm], [bkm])
                S.op("pool", lambda e, gsb_=gsb_: e.memset(gsb_, -1e30), [], [bgs])
                yield
                for i in range(NT):
                    own, var = i // 2, i % 2
                    nk = 256 * own + 128 * (var + 1)
                    nch = nk // 128
                    n += 1
                    sv, bsv = smv[(2 * n + st_) % NSM], b_smv[(2 * n + st_) % NSM]
                    qs = q_[:, i * 128:(i + 1) * 128]
                    sel = own >= 4
                    if sel:
                        mm(PS[ob][:, 128:136], qs, kmb_, True, True, [bq, bkm], [b_PS[ob]])
                        cp("dve", gsb_[:, 0:own], PS[ob][:, 128:128 + own], [b_PS[ob]], [bgs])
                        S.op("dve", lambda e, sv=sv, gsb_=gsb_: e.max(out=sv[:, 0:8], in_=gsb_), [bgs], [bsv])
                        ts("dve", sv[:, 8:8 + own], gsb_[:, 0:own], sv[:, 2:3], -NEG, ALU.is_ge, ALU.mult, [bgs, bsv], [bsv])
                        ts("dve", sv[:, 16:16 + own], sv[:, 8:8 + own], NEG, c31[:, h:h + 1], ALU.add, ALU.add, [bsv, b_small], [bsv])
                        ts("dve", sv[:, 24:24 + own], sv[:, 8:8 + own], NEG, None, ALU.add, None, [bsv], [bsv])
                    for r0 in range(0, nk, 1024):
                        r1 = min(nk, r0 + 1024)
                        for bnk in range((r1 - r0 + 511) // 512):
                            c0 = r0 + bnk * 512
                            w_ = min(512, r1 - c0)
                            mm(PS[sb0 + bnk][:, 0:w_], qs, k_[:, c0:c0 + w_], True, True, [bq, bk], [b_PS[sb0 + bnk]])
                        yield
                        for kb in range(r0 // 256, (r1 + 255) // 256):
                            bnk, off = sb0 + (kb * 256 - r0) // 512, ((kb * 256 - r0) % 512)
                            src = PS[bnk][:, off:off + 256]
                            dst = Sd[:, kb * 256:(kb + 1) * 256]
                            rd = [b_PS[bnk]]
                            if kb == own:
                                w_ = 128 * (var + 1)
                                tt("dve", Sd[:, kb * 256:kb * 256 + w_], PS[bnk][:, off:off + w_], bt_[:, var * 256:var * 256 + w_],
                                   ALU.add, rd + [bb], [bS[kb]])
                            elif kb == own - 1 and var == 0:
                                if sel:
                                    stt(dst, src, sv[:, 24 + kb:25 + kb], bt_[:, 512:768], ALU.add, ALU.add, rd + [bsv, bb], [bS[kb]])
                                else:
                                    tt("dve", dst, src, bt_[:, 512:768], ALU.add, rd + [bb], [bS[kb]])
                            else:
                                col = sv[:, 16 + kb:17 + kb] if sel else c31[:, h:h + 1]
                                kb2 = kb ^ 1
                                special = (kb2 == own) or (kb2 == own - 1 and var == 0)
                                if (not special) and (kb // 2) % 2 == 0:
                                    act(dst, src, AF.Identity, rd + [bsv, b_small], [bS[kb]], bias=col)
                                else:
                                    ts("dve", dst, src, col, None, ALU.add, None, rd + [bsv, b_small], [bS[kb]])
                        yield
                    S.op("dve", lambda e, Sd=Sd, nk=nk, sv=sv: e.tensor_reduce(out=sv[:, 3:4], in_=Sd[:, 0:nk], axis=AX.X, op=ALU.max, negate=True),
                         bS[0:own + 1], [bsv])
                    yield
                    act(Pd[:, 0:nk], Sd[:, 0:nk], AF.Exp, bS[0:own + 1] + [bsv], [bP, bsv], bias=sv[:, 3:4], accum=sv[:, 4:5])
                    S.op("dve", lambda e, sv=sv: e.reciprocal(out=sv[:, 5:6], in_=sv[:, 4:5]), [bsv], [bsv])
                    yield
                    for b8 in range((nch + 7) // 8):
                        psb = PS[pt_i][:, :].bitcast(BF16)
                        n8 = min(8, nch - b8 * 8)
                        for c in range(n8):
                            cc = b8 * 8 + c
                            tr(psb[:, c * 128:(c + 1) * 128], Pd[:, cc * 128:(cc + 1) * 128], ident[:], [bP, b_ident], [b_PS[pt_i]])
                        yield
                        evac(PTd[:, b8 * 8:b8 * 8 + n8, :].rearrange("p a b -> p (a b)"), psb[:, 0:n8 * 128], [b_PS[pt_i]], [bPT[b8]])
                        yield
                    for cc in range(nch):
                        mm(PS[ob][:, 0:128], PTd[:, cc, :], v_[:, cc, :], cc == 0, cc == nch - 1, [bPT[cc // 8], bv], [b_PS[ob]])
                    yield
                    act(yatt[:, i, h * 128:(h + 1) * 128], PS[ob][:, 0:128], AF.Copy, [b_PS[ob], bsv], [b_yatt], scale=sv[:, 5:6])
                    yield

        gens = [stream_gen(0), stream_gen(1)]
        while gens:
            for gen in list(gens):
                try:
                    next(gen)
                except StopIteration:
                    gens.remove(gen)
        S.barrier()
        if dbg:
            S.dma("sp", dbg_yatt.rearrange("(t p) c -> p t c", p=128), yatt, [b_yatt], [], b_yatt)

        yaT = carve(0, [8, L], BF16)
        b_yaT = S.buf("yaT")
        mT = carve(32768, [16, L], BF16)
        b_mT = S.buf("mT")
        o = 98304
        gsb = [carve(o, [L], BF16), carve(o + 4096, [L], BF16)]; o += 8192
        msb = [carve(o, [L], BF16), carve(o + 4096, [L], BF16)]; o += 8192
        b_gsb = S.bufs(2, "gsb2"); b_msb = S.bufs(2, "msb")
        tq_ = [carve(o, [512], F32), carve(o + 2048, [512], F32)]; o += 4096
        b_tq = S.bufs(2, "tq")
        for i in range(NT):
            ps, bps = next_ps(0, 8)
            psb = ps[:, :].bitcast(BF16)
            for h in range(8):
                tr(psb[:, h * 128:(h + 1) * 128], yatt[:, i, h * 128:(h + 1) * 128], ident[:], [b_yatt, b_ident], [bps])
            evac(yaT[:, :, i * 128:(i + 1) * 128], psb.rearrange("p (a b) -> p a b", b=128), [bps], [b_yaT])
        S.barrier()
        n_ = 0
        for half in range(4):
            wslab, wb = wget()
            for fl in range(4):
                ft = half * 4 + fl
                i2 = ft % 2
                S.dma("sp", gsb[i2], gbT_d[ft * 128:(ft + 1) * 128, :], [], [b_gsb[i2]], b_gsb[i2])
                S.dma("sp", msb[i2], mssm_d[ft * 128:(ft + 1) * 128, :], [], [b_msb[i2]], b_msb[i2])
                for c in range(4):
                    cs_ = slice(c * 512, (c + 1) * 512)
                    ps, bps = next_ps(0, 8)
                    for kt in range(8):
                        mm(ps[:, :], wslab[:, kt, fl * 128:(fl + 1) * 128], yaT[:, kt, cs_], kt == 0, kt == 7, [wb, b_yaT], [bps])
                    j2 = n_ % 2
                    n_ += 1
                    tt("dve", tq_[j2], ps[:, :], gsb[i2][:, cs_], ALU.mult, [bps, b_gsb[i2]], [b_tq[j2]])
                    tt("dve", mT[:, ft, cs_], tq_[j2], msb[i2][:, cs_], ALU.add, [b_tq[j2], b_msb[i2]], [b_mT])
        S.barrier()

        o = 0
        xin = [carve(o + i * 2048, [512], F32) for i in range(3)]; o += 6144
        xo = [carve(o + i * 2048, [512], F32) for i in range(3)]; o += 6144
        b_xin = S.bufs(3, "xin"); b_xo = S.bufs(3, "xo")
        assert o <= 32768
        n_ = 0
        for cs4 in range(4):
            wslab, wb = wget()
            csl = slice(cs4 * 512, (cs4 + 1) * 512)
            for ti in range(NT):
                j3 = n_ % 3
                n_ += 1
                S.dma("sp", xin[j3], x_d[ti * 128:(ti + 1) * 128, csl], [], [b_xin[j3]], b_xin[j3])
                ps, bps = next_ps(0, 8)
                for kt in range(16):
                    mm(ps[:, :], mT[:, kt, ti * 128:(ti + 1) * 128], wslab[:, kt, :], kt == 0, kt == 15, [wb, b_mT], [bps])
                tt("dve", xo[j3], ps[:, :], gbc[:, 0, csl], ALU.mult, [bps, b_gbc[0]], [b_xo[j3]])
                tt("dve", xo[j3], xo[j3], xin[j3], ALU.add, [b_xo[j3], b_xin[j3]], [b_xo[j3]])
                S.dma("sp", x1_d[ti * 128:(ti + 1) * 128, csl], xo[j3], [b_xo[j3]], [], b_xo[j3])
        S.barrier()
        if stop_after <= 5:
            S.barrier()
            S.emit()
            return nc

        o = 0
        X1 = [carve(o + i * 8192, [D], F32) for i in range(4)]; o += 32768
        b_X1 = S.bufs(4, "X1")
        h2T = carve(o, [16, 512], BF16); o += 16384
        b_h2 = S.bufs(16, "h2T")
        fT = carve(o, [64, 512], BF16); o += 65536
        b_fT = S.buf("fT")
        rt = [carve(o, [512], BF16), carve(o + 1024, [512], BF16)]; o += 2048
        b_rt = S.bufs(2, "rt")
        junk = carve(o, [D], BF16); o += 4096
        b_junk = S.buf("junk2")
        xn = [carve(o, [D], BF16), carve(o + 4096, [D], BF16)]; o += 8192
        b_xn = S.bufs(2, "xn2")
        tq_ = [carve(o, [512], F32), carve(o + 2048, [512], F32)]; o += 4096
        b_tq = S.bufs(2, "tq2")
        S.dma("sp", gbc[:, 0, :], gfin_d, [], [b_gbc[0]], b_gbc[0])
        stat2 = misc[:, 224:480]
        b_stat2 = S.bufs(32, "stat2")
        n_ = 0
        for tc in range(4):
            for j in range(4):
                ti = tc * 4 + j
                S.dma("sp", X1[j], x1_d[ti * 128:(ti + 1) * 128, :], [], [b_X1[j]], b_X1[j])
            for j in range(4):
                ti = tc * 4 + j
                sc = stat2[:, 4 * ti:4 * ti + 4]
                norm_tile(X1[j], b_X1[j], ti, xn[j % 2], b_xn[j % 2], sc, b_stat2[ti])
                transpose_mod(xn[j % 2], b_xn[j % 2], h2T, b_h2, slice(j * 128, (j + 1) * 128), A2, sh2, b_modT2)
            for fs in range(16):
                wslab, wb = wget()
                for jj in range(4):
                    fi = fs * 4 + jj
                    ps, bps = next_ps(0, 4)
                    for kt in range(16):
                        mm(ps[:, :], wslab[:, kt, jj * 128:(jj + 1) * 128], h2T[:, kt, :], kt == 0, kt == 15, [wb, b_h2[kt]], [bps])
                    j2 = n_ % 2
                    n_ += 1
                    act(rt[j2], ps[:, :], AF.Relu, [bps], [b_rt[j2]])
                    tt("dve", fT[:, fi, :], rt[j2], rt[j2], ALU.mult, [b_rt[j2]], [b_fT])
            for cs4 in range(4):
                csl = slice(cs4 * 512, (cs4 + 1) * 512)
                base = 4 if cs4 % 2 == 0 else 0
                for kg in range(4):
                    wslab, wb = wget()
                    for kk in range(16):
                        kt = kg * 16 + kk
                        for j in range(4):
                            mm(PS[base + j][:, :], fT[:, kt, j * 128:(j + 1) * 128], wslab[:, kk, :], kt == 0, kt == 63,
                               [wb, b_fT], [b_PS[base + j]])
                for j in range(4):
                    j2 = n_ % 2
                    n_ += 1
                    tt("dve", tq_[j2], PS[base + j][:, :], gbc[:, 1, csl], ALU.mult, [b_PS[base + j], b_gbc[1]], [b_tq[j2]])
                    tt("dve", X1[j][:, csl], X1[j][:, csl], tq_[j2], ALU.add, [b_X1[j], b_tq[j2]], [b_X1[j]])
            for j in range(4):
                ti = tc * 4 + j
                sc = stat2[:, 64 + 4 * ti:64 + 4 * ti + 4]
                bst = b_stat2[16 + ti]
                act(junk, X1[j], AF.Square, [b_X1[j]], [b_junk, bst], accum=sc[:, 0:1])
                ts("dve", sc[:, 1:2], sc[:, 0:1], 1.0 / D, EPS, ALU.mult, ALU.add, [bst], [bst])
                act(sc[:, 2:3], sc[:, 1:2], AF.Sqrt, [bst], [bst])
                S.op("dve", lambda e, sc=sc: e.reciprocal(out=sc[:, 3:4], in_=sc[:, 2:3]), [bst], [bst])
                stt(X1[j], X1[j], sc[:, 3:4], gbc[:, 0, :], ALU.mult, ALU.mult, [b_X1[j], bst, b_gbc[0]], [b_X1[j]])
                S.dma("sp", out_d[ti * 128:(ti + 1) * 128, :], X1[j], [b_X1[j]], [], b_X1[j])
        S.barrier()
        S.emit()
    return nc


_CACHE = {}


def kernel(**inputs):
    inp = {k: np.asarray(v) for k, v in inputs.items()}
    if "nc" not in _CACHE:
        _CACHE["nc"] = build_program()
    nc = _CACHE["nc"]
    sh = _host_shared(inp)
    in_maps = []
    for b in range(8):
        m = dict(sh)
        m.update(_host_core(inp, b))
        in_maps.append(m)
    res = run_bass_kernel_spmd(nc, in_maps, core_ids=list(range(8)))
    out = np.stack([np.asarray(r["out"]) for r in res.results], 0)
    return out.astype(np.float32)
```

```python
import math
from contextlib import ExitStack

import numpy as np
import concourse.bass as bass
import concourse.mybir as mybir
from concourse.bass_utils import run_bass_kernel_spmd

F32 = mybir.dt.float32
BF16 = mybir.dt.bfloat16
AF = mybir.ActivationFunctionType
ALU = mybir.AluOpType
AX = mybir.AxisListType

ENGS = ["pe", "act", "dve", "pool", "sp"]

L = 2048
D = 2048
NT = 16
DFF = 8192
EPS = 1e-6
NEG = -30000.0
TWO_PI = 2.0 * math.pi
PWS = [1, 2, 3, 4, 8, 12, 16, 32, 48, 64, 128, 192, 256, 512, 768, 1024]
LEVELS = [(1, [1, 2, 3]), (4, [1, 2, 3]), (16, [1, 2, 3]), (64, [1, 2, 3]), (256, [1, 2, 3]), (1024, [1])]


class Buf:
    __slots__ = ("name", "lw", "rd", "sem", "semval", "lastdma")

    def __init__(self, name):
        self.name = name
        self.lw = None
        self.rd = []
        self.sem = None
        self.semval = 0
        self.lastdma = None


class Op:
    __slots__ = ("eng", "fn", "deps", "signal", "ev_sem", "ev_val", "is_dma")

    def __init__(self, eng, fn, is_dma=False):
        self.eng = eng
        self.fn = fn
        self.deps = set()
        self.signal = False
        self.ev_sem = None
        self.ev_val = 0
        self.is_dma = is_dma


class Sched:
    def __init__(self, nc, stack, same_engine_sync=True):
        self.nc = nc
        self.stack = stack
        self.ops = {e: [] for e in ENGS}
        self.same = same_engine_sync
        self.esem = {e: stack.enter_context(nc.semaphore("es_" + e)) for e in ENGS}
        self.dma_bufs = []
        self.nb = 0

    def buf(self, name=None):
        self.nb += 1
        return Buf(name or ("b%d" % self.nb))

    def bufs(self, n, name="b"):
        return [self.buf("%s%d" % (name, i)) for i in range(n)]

    def _record(self, o, reads, writes):
        deps = o.deps
        for b in reads:
            if b.lw is not None:
                deps.add(b.lw)
        for b in writes:
            if b.lw is not None:
                deps.add(b.lw)
            for r in b.rd:
                if r.eng != o.eng or r.is_dma or o.is_dma:
                    deps.add(r)
        for b in reads:
            if not o.is_dma:
                b.rd = [r for r in b.rd if r.is_dma or r.eng != o.eng]
            b.rd.append(o)
        for b in writes:
            b.lw = o
            b.rd = []
        self.ops[o.eng].append(o)

    def op(self, eng, fn, reads=(), writes=()):
        o = Op(eng, fn)
        self._record(o, reads, writes)
        return o

    def dma(self, eng, out, in_, reads, writes, sembuf):
        if sembuf.sem is None:
            sembuf.sem = self.stack.enter_context(self.nc.semaphore("ds%d_%s" % (len(self.dma_bufs), sembuf.name)))
            self.dma_bufs.append(sembuf)
        o = Op(eng, lambda e: e.dma_start(out=out, in_=in_), is_dma=True)
        if sembuf.lastdma is not None:
            o.deps.add(sembuf.lastdma)
        sembuf.lastdma = o
        sembuf.semval += 16
        o.ev_sem = sembuf.sem
        o.ev_val = sembuf.semval
        o.signal = True
        self._record(o, reads, writes)
        return o

    def barrier(self):
        lasts = []
        for e in ENGS:
            for o in reversed(self.ops[e]):
                if not o.is_dma and o.fn is not None:
                    lasts.append(o)
                    break
        dmas = [b.lastdma for b in self.dma_bufs if b.lastdma is not None]
        for e in ENGS:
            o = Op(e, None)
            for l in lasts:
                if l.eng != e:
                    o.deps.add(l)
            for d in dmas:
                o.deps.add(d)
            self.ops[e].append(o)

    def _skip(self, o, d):
        if d.is_dma or o.is_dma or d.eng != o.eng:
            return False
        return d.eng == "pe" or not self.same

    def emit(self):
        for e in ENGS:
            for o in self.ops[e]:
                for d in o.deps:
                    if not d.is_dma and not self._skip(o, d):
                        d.signal = True
        for e in ENGS:
            c = 0
            for o in self.ops[e]:
                if not o.is_dma and o.signal:
                    c += 1
                    o.ev_sem = self.esem[e]
                    o.ev_val = c
            self.maxcount = getattr(self, "maxcount", {})
            self.maxcount[e] = (c, len(self.ops[e]))
        import os as _os
        if _os.environ.get("SCHED_VERBOSE"):
            print("sched counts (signals, ops):", self.maxcount, "dma sems:", {b.name: b.semval for b in self.dma_bufs})
        nc = self.nc
        engobj = {"pe": "tensor", "act": "scalar", "dve": "vector", "pool": "gpsimd", "sp": "sync"}
        with nc.Block() as block:
            for e in ENGS:
                def body(eng, ops=self.ops[e], e=e):
                    waited = {}
                    for o in ops:
                        need = {}
                        for d in o.deps:
                            if self._skip(o, d):
                                continue
                            k = id(d.ev_sem)
                            if k not in need or need[k][1] < d.ev_val:
                                need[k] = (d.ev_sem, d.ev_val)
                        for k, (s, v) in need.items():
                            if waited.get(k, 0) >= v:
                                continue
                            waited[k] = v
                            eng.wait_ge(s, v)
                        if o.fn is None:
                            continue
                        ins = o.fn(eng)
                        if o.is_dma:
                            ins.then_inc(o.ev_sem, 16)
                        elif o.signal:
                            ins.then_inc(o.ev_sem, 1)

                getattr(block, engobj[e])(body)


def apm(ap, pattern):
    return bass.AP(ap.tensor, ap.offset, [list(p) for p in pattern])


def _t5_bucket_np(rel):
    n = np.maximum(rel, 0)
    nf = np.maximum(n, 1).astype(np.float32)
    large = 16 + (np.log(nf / np.float32(16.0)) / np.float32(math.log(128 / 16)) * np.float32(16.0)).astype(np.int32)
    large = np.minimum(large, 31)
    return np.where(n < 16, n, large)


def _host_shared(inp):
    f = np.float32
    sh = {}
    sh["w_ada"] = np.ascontiguousarray(inp["w_ada"][0])
    sh["w_in"] = np.ascontiguousarray(inp["w_in"][0])
    sh["w_glu"] = np.ascontiguousarray(inp["w_glu"][0])
    sh["w_pssm"] = np.ascontiguousarray(inp["w_proj_ssm"][0])
    sh["w_patt"] = np.ascontiguousarray(inp["w_proj_attn"][0])
    sh["w_out"] = np.ascontiguousarray(inp["w_out"][0])
    sh["w_ff1"] = np.ascontiguousarray(inp["w_ff1"][0])
    sh["w_ff2"] = np.ascontiguousarray(inp["w_ff2"][0])
    sm = np.zeros((128, 512), f)
    sm[:, 0:96] = inp["b_ada"][0].reshape(96, 128).T
    sm[:, 96:112] = inp["norm_mix_g"][0].reshape(16, 128).T
    sm[:, 112:128] = inp["norm_mlp_g"][0].reshape(16, 128).T
    sm[:, 128:136] = inp["ssm_d"][0].reshape(8, 128).T
    sm[:, 136:144] = inp["b_glu"][0].reshape(8, 128).T
    sm[:, 144:152] = np.broadcast_to(inp["rel_bias"][31][None, :], (128, 8))
    sm[:64, 152] = -1.0
    sm[64:, 152] = 1.0
    sm[:64, 153] = 1.0
    sm[64:, 153] = -1.0
    for gl in range(8):
        sm[16 * gl:16 * gl + 16, 160 + gl] = 1.0
    sh["small"] = sm
    sh["gfin"] = np.ascontiguousarray(np.broadcast_to(inp["norm_final_g"][None, :], (128, D))).astype(f)
    are = inp["ssm_a_re"][0].T
    aim = inp["ssm_a_im"][0].T
    sp = np.zeros((128, 192), f)
    sp[:, 0:64] = np.concatenate([are, are], 0)
    sp[:, 64:128] = np.concatenate([aim, aim], 0)
    sp[:, 128:192] = np.broadcast_to(inp["ssm_log_dt"][0][None, :], (128, 64))
    sh["ssm_p"] = sp
    bre = inp["ssm_b_re"][0].transpose(1, 0, 2)
    bim = inp["ssm_b_im"][0].transpose(1, 0, 2)
    cre = inp["ssm_c_re"][0].transpose(2, 0, 1)
    cim = inp["ssm_c_im"][0].transpose(2, 0, 1)
    sb = np.zeros((128, 3, 64, 16), f)
    sb[:, 0] = np.concatenate([bre, bim], 0)
    sb[:, 1] = np.concatenate([bim, bre], 0)
    sb[:, 2] = np.concatenate([cre, cim], 0)
    sh["ssm_bc"] = sb.reshape(128, 3 * 64 * 16)
    cm = np.zeros((128, 3, 128), f)
    cm[:, 0] = np.eye(128, dtype=f)
    cm[:, 1] = np.roll(np.eye(128, dtype=f), 64, axis=1)
    cm[:, 2] = 1.0
    sh["cmat"] = cm.reshape(128, 384)
    rb = inp["rel_bias"]
    q = np.arange(128)[:, None]
    kk = np.arange(256)[None, :]
    bt = np.zeros((8, 128, 768), f)
    cmask = np.zeros((128, 512), f)
    for var in range(2):
        rel = 128 * var + q - kk
        bt[:, :, var * 256:(var + 1) * 256] = rb[_t5_bucket_np(rel)].transpose(2, 0, 1)
        cmask[:, var * 256:(var + 1) * 256] = np.where(rel >= 0, 0.0, NEG)
    rel = 256 + q - kk
    bt[:, :, 512:768] = rb[_t5_bucket_np(rel)].transpose(2, 0, 1)
    sh["bias_t"] = bt
    sh["cmask"] = cmask
    return sh


def _host_core(inp, b):
    return {
        "x": np.ascontiguousarray(inp["x"][b]),
        "cT": np.ascontiguousarray(inp["c"][b].reshape(16, 128).T),
    }


def build_program(dbg=False, stop_after=99, same_engine_sync=True):
    nc = bass.Bass("TRN2", target_bir_lowering=False)

    def din(name, shape, dt=F32):
        return nc.dram_tensor(name, list(shape), dt, kind="ExternalInput").ap()

    def dscr(name, shape, dt):
        if dbg:
            return nc.dram_tensor(name, list(shape), dt, kind="ExternalOutput").ap()
        return nc.dram_tensor(name, list(shape), dt).ap()

    x_d = din("x", [L, D])
    cT_d = din("cT", [128, 16])
    w_ada_d = din("w_ada", [D, 6 * D])
    w_in_d = din("w_in", [D, 8192])
    w_glu_d = din("w_glu", [1024, 1024])
    w_pssm_d = din("w_pssm", [1024, D])
    w_patt_d = din("w_patt", [1024, D])
    w_out_d = din("w_out", [D, D])
    w_ff1_d = din("w_ff1", [D, DFF])
    w_ff2_d = din("w_ff2", [DFF, D])
    small_d = din("small", [128, 512])
    gfin_d = din("gfin", [128, D])
    ssm_p_d = din("ssm_p", [128, 192])
    ssm_bc_d = din("ssm_bc", [128, 3072])
    cmat_d = din("cmat", [128, 384])
    bias_t_d = din("bias_t", [8, 128, 768])
    cmask_d = din("cmask", [128, 512])
    out_d = nc.dram_tensor("out", [L, D], F32, kind="ExternalOutput").ap()

    uT_d = dscr("uT_s", [1024, L], BF16)
    qT_d = dscr("qT_s", [1024, L], BF16)
    kT_d = dscr("kT_s", [1024, L], BF16)
    v_d = dscr("v_s", [L, 1024], BF16)
    gaT_d = dscr("gaT_s", [D, L], BF16)
    gbT_d = dscr("gbT_s", [D, L], BF16)
    mssm_d = dscr("mssm_s", [D, L], BF16)
    x1_d = dscr("x1_s", [L, D], F32)
    if dbg:
        dbg_modT = nc.dram_tensor("dbg_modT", [128, 96], F32, kind="ExternalOutput").ap()
        dbg_yssm = nc.dram_tensor("dbg_yssm", [1024, L], BF16, kind="ExternalOutput").ap()
        dbg_yatt = nc.dram_tensor("dbg_yatt", [L, 1024], BF16, kind="ExternalOutput").ap()

    with ExitStack() as st:
        S = Sched(nc, st, same_engine_sync)
        ent = st.enter_context

        small = ent(nc.sbuf_tensor("small_sb", [128, 512], F32))
        cmat = ent(nc.sbuf_tensor("cmat_sb", [128, 384], F32))
        ident = ent(nc.sbuf_tensor("ident", [128, 128], BF16))
        misc = ent(nc.sbuf_tensor("misc", [128, 512], F32))
        gbc = ent(nc.sbuf_tensor("gbc", [128, 2, D], F32))
        wsl = [ent(nc.sbuf_tensor("wsl%d" % i, [128, 8192], BF16)) for i in range(3)]
        NA = 136000
        arena = ent(nc.sbuf_tensor("arena", [128, NA // 2], BF16))
        PS = [ent(nc.psum_tensor("ps%d" % i, [128, 512], F32)) for i in range(8)]
        b_PS = S.bufs(8, "ps")
        b_wsl = S.bufs(3, "wsl")
        b_small = S.buf("small")
        b_cmat = S.buf("cmat")
        b_ident = S.buf("ident")
        b_gbc = S.bufs(2, "gbc")

        identf = cmat[:, 0:128]
        swapf = cmat[:, 128:256]
        onesf = cmat[:, 256:384]

        def carve(off, shape, dt):
            n = int(np.prod(shape))
            esz = 2 if dt == BF16 else 4
            assert off % 4 == 0 and off + n * esz <= NA, (off, shape, NA)
            ap = arena[:, off // 2: off // 2 + n * esz // 2]
            if dt != BF16:
                ap = ap.bitcast(dt)
            if len(shape) == 2:
                ap = ap.rearrange("p (a b) -> p a b", b=shape[1])
            elif len(shape) == 3:
                ap = ap.rearrange("p (a b c) -> p a b c", b=shape[1], c=shape[2])
            return ap

        def mm(out, lhsT, rhs, start, stop, reads, writes):
            S.op("pe", lambda e: e.matmul(out, lhsT=lhsT, rhs=rhs, start=start, stop=stop), reads, writes)

        def tr(out, in_, idn, reads, writes):
            S.op("pe", lambda e: e.transpose(out=out, in_=in_, identity=idn), reads, writes)

        def act(out, in_, func, reads, writes, bias=None, scale=None, accum=None):
            kw = {}
            if bias is not None:
                kw["bias"] = bias
            if scale is not None:
                kw["scale"] = scale
            if accum is not None:
                kw["accum_out"] = accum
            S.op("act", lambda e: e.activation(out=out, in_=in_, func=func, **kw), reads, writes)

        def ts(eng, out, in0, s1, s2, op0, op1, reads, writes):
            if op1 is None:
                S.op(eng, lambda e: e.tensor_scalar(out=out, in0=in0, scalar1=s1, scalar2=None, op0=op0), reads, writes)
            else:
                S.op(eng, lambda e: e.tensor_scalar(out=out, in0=in0, scalar1=s1, scalar2=s2, op0=op0, op1=op1), reads, writes)

        def tt(eng, out, in0, in1, op, reads, writes):
            S.op(eng, lambda e: e.tensor_tensor(out=out, in0=in0, in1=in1, op=op), reads, writes)

        def stt(out, in0, scalar, in1, op0, op1, reads, writes):
            S.op("dve", lambda e: e.scalar_tensor_tensor(out=out, in0=in0, scalar=scalar, in1=in1, op0=op0, op1=op1), reads, writes)

        def cp(eng, out, in_, reads, writes):
            if eng == "act":
                act(out, in_, AF.Copy, reads, writes)
            else:
                S.op(eng, lambda e: e.tensor_copy(out=out, in_=in_), reads, writes)

        evc = [0]

        def evac(out, in_, reads, writes):
            evc[0] += 1
            cp("act" if evc[0] % 2 else "dve", out, in_, reads, writes)

        psr = [0]

        def next_ps(lo=0, hi=8):
            psr[0] += 1
            i = lo + psr[0] % (hi - lo)
            return PS[i], b_PS[i]

        slabs = []
        wstate = {"issued": 0, "used": 0}

        def wview(w, k0, nk, c0, ncol):
            return w.rearrange("(k p) c -> p k c", p=128)[:, k0:k0 + nk, c0:c0 + ncol]

        def plan_slabs():
            for s in range(8):
                slabs.append((wview(w_ada_d, 0, 16, s * 512, 512), 16, 512))
            for s in range(16):
                slabs.append((wview(w_in_d, 0, 16, s * 512, 512), 16, 512))
                slabs.append((wview(w_ada_d, 0, 16, (8 + s) * 512, 512), 16, 512))
            for s in range(2):
                slabs.append((wview(w_glu_d, 0, 8, s * 512, 512), 8, 512))
            for s in range(4):
                slabs.append((wview(w_pssm_d, 0, 8, s * 512, 512), 8, 512))
            for s in range(4):
                slabs.append((wview(w_patt_d, 0, 8, s * 512, 512), 8, 512))
            for s in range(4):
                slabs.append((wview(w_out_d, 0, 16, s * 512, 512), 16, 512))
            for tc in range(4):
                for s in range(16):
                    slabs.append((wview(w_ff1_d, 0, 16, s * 512, 512), 16, 512))
                for cs in range(4):
                    for kg in range(4):
                        slabs.append((wview(w_ff2_d, kg * 16, 16, cs * 512, 512), 16, 512))

        plan_slabs()

        import os as _os2
        _wcap = int(_os2.environ.get("WCAP", "100000"))

        def wissue_upto(n):
            while wstate["issued"] < min(n, len(slabs), _wcap):
                i = wstate["issued"]
                view, nk, ncol = slabs[i]
                sl = wsl[i % 3][:, 0:nk * ncol].rearrange("p (k c) -> p k c", c=ncol)
                S.dma("pool", sl, view, [], [b_wsl[i % 3]], b_wsl[i % 3])
                wstate["issued"] += 1

        def wget():
            i = wstate["used"]
            wissue_upto(i + 3)
            wstate["used"] += 1
            view, nk, ncol = slabs[i]
            sl = wsl[i % 3][:, 0:nk * ncol].rearrange("p (k c) -> p k c", c=ncol)
            return sl, b_wsl[i % 3]

        S.dma("sp", small[:], small_d, [], [b_small], b_small)
        S.dma("sp", cmat[:], cmat_d, [], [b_cmat], b_cmat)
        cp("dve", ident[:], identf, [b_cmat], [b_ident])
        b_misc = S.buf("misc")
        b_modT = S.buf("modT")
        cT_sb = misc[:, 0:16]
        cact = misc[:, 16:24].bitcast(BF16)
        modT = misc[:, 32:128]
        A1 = misc[:, 128:144]
        A2 = misc[:, 144:160]
        S.dma("sp", cT_sb, cT_d, [], [b_misc], b_misc)
        wissue_upto(3)

        act(cact, cT_sb, AF.Silu, [b_misc], [b_misc])
        modps, b_modps = PS[7], b_PS[7]
        def mod_slab(s):
            wslab, wb = wget()
            for jj in range(4):
                j = s * 4 + jj
                for kt in range(16):
                    mm(modps[:, j:j + 1], wslab[:, kt, jj * 128:(jj + 1) * 128], cact[:, kt:kt + 1],
                       kt == 0, kt == 15, [wb, b_misc], [b_modps])

        for s in range(8):
            mod_slab(s)
        b_modT2 = S.buf("modT2")
        tt("dve", modT[:, 0:32], modps[:, 0:32], small[:, 0:32], ALU.add, [b_modps, b_small], [b_modT])
        stt(A1, modT[:, 16:32], 1.0, small[:, 96:112], ALU.add, ALU.mult, [b_modT, b_small], [b_modT])
        sh1 = modT[:, 0:16]
        sh2 = modT[:, 48:64]

        def mod_finish():
            tt("dve", modT[:, 32:96], modps[:, 32:96], small[:, 32:96], ALU.add, [b_modps, b_small], [b_modT2])
            stt(A2, modT[:, 64:80], 1.0, small[:, 112:128], ALU.add, ALU.mult, [b_modT2, b_small], [b_modT2])
            if dbg:
                S.dma("sp", dbg_modT, modT, [b_modT, b_modT2], [], b_modT2)
        Gt = [ent(nc.sbuf_tensor("Gt%d" % i, [128, 128], F32)) for i in range(2)]
        b_Gt = S.bufs(2, "Gt")

        def bcast_gates():
            for which, c0 in ((0, 32), (1, 80)):
                for grp in range(4):
                    ps, bps = next_ps(0, 6)
                    for jj in range(4):
                        t = grp * 4 + jj
                        gi = t % 2
                        ts("dve", Gt[gi][:], onesf, modT[:, c0 + t:c0 + t + 1], None, ALU.mult, None, [b_modT2, b_cmat], [b_Gt[gi]])
                        mm(ps[:, jj * 128:(jj + 1) * 128], Gt[gi][:], identf, True, True, [b_Gt[gi], b_cmat], [bps])
                    evac(gbc[:, which, grp * 512:(grp + 1) * 512], ps[:, :], [bps], [b_gbc[which]])

        if stop_after <= 0:
            for s in range(16):
                wget()
                mod_slab(8 + s)
            mod_finish()
            S.barrier()
            S.emit()
            return nc

        hT = carve(0, [16, L], BF16)
        b_hT = S.bufs(16, "hT")
        o = 65536
        xt = [carve(o, [D], F32), carve(o + 8192, [D], F32)]
        b_xt = S.bufs(2, "xt")
        o += 16384
        junk = carve(o, [D], BF16)
        b_junk = S.buf("junk")
        o += 4096
        xn = [carve(o, [D], BF16), carve(o + 4096, [D], BF16)]
        b_xn = S.bufs(2, "xn")
        o += 8192
        stg = [carve(o + i * 4096, [L], BF16) for i in range(3)]
        b_stg = [S.bufs(4, "stg%d_" % i) for i in range(3)]
        o += 12288
        stat = misc[:, 160:224]
        b_stat = S.bufs(16, "stat")

        def norm_tile(src, bsrc, tt_i, xn_ap, b_xn_i, statcol, bst):
            act(junk, src, AF.Square, [bsrc], [b_junk, bst], accum=statcol[:, 0:1])
            ts("dve", statcol[:, 1:2], statcol[:, 0:1], 1.0 / D, EPS, ALU.mult, ALU.add, [bst], [bst])
            act(statcol[:, 2:3], statcol[:, 1:2], AF.Sqrt, [bst], [bst])
            S.op("dve", lambda e: e.reciprocal(out=statcol[:, 3:4], in_=statcol[:, 2:3]), [bst], [bst])
            ts("dve", xn_ap, src, statcol[:, 3:4], None, ALU.mult, None, [bsrc, bst], [b_xn_i])

        def transpose_mod(xn_ap, b_xn_i, dst, b_dst, tokslice, Acol, shcol, b_mod):
            for fg in range(4):
                ps, bps = next_ps(0, 7)
                psb = ps[:, :].bitcast(BF16)
                for j in range(4):
                    ft = fg * 4 + j
                    tr(psb[:, j * 128:(j + 1) * 128], xn_ap[:, ft * 128:(ft + 1) * 128], ident[:], [b_xn_i, b_ident], [bps])
                for j in range(4):
                    ft = fg * 4 + j
                    if fg % 2 == 0:
                        ts("dve", dst[:, ft, tokslice], psb[:, j * 128:(j + 1) * 128], Acol[:, ft:ft + 1], shcol[:, ft:ft + 1],
                           ALU.mult, ALU.add, [bps, b_mod], [b_dst[ft]])
                    else:
                        act(dst[:, ft, tokslice], psb[:, j * 128:(j + 1) * 128], AF.Identity, [bps, b_mod], [b_dst[ft]],
                            bias=shcol[:, ft:ft + 1], scale=Acol[:, ft:ft + 1])

        S.dma("sp", xt[0], x_d[0:128, :], [], [b_xt[0]], b_xt[0])
        for ti in range(NT):
            if ti + 1 < NT:
                S.dma("sp", xt[(ti + 1) % 2], x_d[(ti + 1) * 128:(ti + 2) * 128, :], [], [b_xt[(ti + 1) % 2]], b_xt[(ti + 1) % 2])
            sc = stat[:, 4 * ti:4 * ti + 4]
            norm_tile(xt[ti % 2], b_xt[ti % 2], ti, xn[ti % 2], b_xn[ti % 2], sc, b_stat[ti])
            transpose_mod(xn[ti % 2], b_xn[ti % 2], hT, b_hT, slice(ti * 128, (ti + 1) * 128), A1, sh1, b_modT)

        if stop_after <= 1:
            if dbg:
                for kt in range(16):
                    S.dma("sp", gaT_d[kt * 128:(kt + 1) * 128, :], hT[:, kt, :], [b_hT[kt]], [], b_hT[kt])
            S.barrier()
            S.emit()
            return nc
        gflat = gbc[:, :, :].rearrange("p a b -> p (a b)")
        sp_ = gflat[:, 0:192]
        tmp = [gflat[:, 192 + 64 * i:256 + 64 * i] for i in range(28)]
        pwre = gflat[:, 2048:3072].rearrange("p (a b) -> p a b", b=64)
        pwim = gflat[:, 3072:4096].rearrange("p (a b) -> p a b", b=64)
        b_sp = S.buf("ssm_p")
        b_t = S.buf("ssmtmp")
        b_pw = S.buf("pw")
        are, aim, ldt = sp_[:, 0:64], sp_[:, 64:128], sp_[:, 128:192]
        (dt_, th, lm, mag, phs, phc, tq, sn, cs, ar, ai, pr, den, rden, fre, fim, fims, t1, t2, t3, t4, thc) = tmp[:22]
        R_ = [b_t, b_sp]
        W_ = [b_t]

        def ssm_chain():
            S.dma("sp", sp_, ssm_p_d, [], [b_sp], b_sp)
            act(dt_, ldt, AF.Exp, [b_sp], W_)
            tt("dve", th, dt_, aim, ALU.mult, R_, W_)
            tt("dve", lm, dt_, are, ALU.mult, R_, W_)
            act(mag, lm, AF.Exp, R_, W_)
            ts("dve", thc, th, math.pi / 2, None, ALU.add, None, R_, W_)
            yield
            for src, dst in ((th, phs), (thc, phc)):
                cp("dve", dst, src, R_, W_)
                for m in range(5):
                    thr = (2 * m + 1) * math.pi
                    ts("dve", tq, src, thr, -TWO_PI, ALU.is_gt, ALU.mult, R_, W_)
                    tt("dve", dst, dst, tq, ALU.add, R_, W_)
                    yield
            act(sn, phs, AF.Sin, R_, W_)
            act(cs, phc, AF.Sin, R_, W_)
            yield
            tt("dve", ar, mag, cs, ALU.mult, R_, W_)
            tt("dve", ai, mag, sn, ALU.mult, R_, W_)
            ts("dve", pr, ar, -1.0, None, ALU.add, None, R_, W_)
            yield
            tt("dve", t1, are, are, ALU.mult, R_, W_)
            tt("dve", t2, aim, aim, ALU.mult, R_, W_)
            tt("dve", den, t1, t2, ALU.add, R_, W_)
            S.op("dve", lambda e: e.reciprocal(out=rden, in_=den), R_, W_)
            yield
            tt("dve", t1, pr, are, ALU.mult, R_, W_)
            tt("dve", t2, ai, aim, ALU.mult, R_, W_)
            tt("dve", t3, t1, t2, ALU.add, R_, W_)
            tt("dve", fre, t3, rden, ALU.mult, R_, W_)
            yield
            tt("dve", t1, ai, are, ALU.mult, R_, W_)
            tt("dve", t2, pr, aim, ALU.mult, R_, W_)
            tt("dve", t3, t1, t2, ALU.subtract, R_, W_)
            tt("dve", fim, t3, rden, ALU.mult, R_, W_)
            ts("dve", fims, fim, small[:, 152:153], None, ALU.mult, None, R_ + [b_small], W_)
            yield
            cp("dve", pwre[:, 0, :], ar, R_, [b_pw])
            cp("dve", pwim[:, 0, :], ai, R_, [b_pw])
            for k in range(1, 16):
                p = PWS[k]
                best = None
                for a_ in range(k):
                    for b_ in range(a_, k):
                        if PWS[a_] + PWS[b_] == p:
                            best = (a_, b_)
                a_, b_ = best
                tt("dve", t1, pwre[:, a_, :], pwre[:, b_, :], ALU.mult, [b_pw, b_t], W_)
                tt("dve", t2, pwim[:, a_, :], pwim[:, b_, :], ALU.mult, [b_pw, b_t], W_)
                tt("dve", pwre[:, k, :], t1, t2, ALU.subtract, [b_t], [b_pw])
                yield
                tt("dve", t3, pwre[:, a_, :], pwim[:, b_, :], ALU.mult, [b_pw, b_t], W_)
                tt("dve", t4, pwim[:, a_, :], pwre[:, b_, :], ALU.mult, [b_pw, b_t], W_)
                tt("dve", pwim[:, k, :], t3, t4, ALU.add, [b_t], [b_pw])
                yield
            ts("dve", pwim.rearrange("p a b -> p (a b)"), pwim.rearrange("p a b -> p (a b)"), small[:, 153:154], None, ALU.mult, None,
               [b_pw, b_small], [b_pw])

        chain = ssm_chain()

        def step_chain():
            try:
                next(chain)
            except StopIteration:
                pass

        QSCALE = 128.0 ** -0.5
        sidx = [0]
        p2n = [0]

        def form2_slab(dst_d, row0, post):
            wslab, wb = wget()
            for ct in range(4):
                si = sidx[0] % 3
                sidx[0] += 1
                for c in range(4):
                    ps, bps = next_ps(0, 7)
                    for kt in range(16):
                        mm(ps[:, :], wslab[:, kt, ct * 128:(ct + 1) * 128], hT[:, kt, c * 512:(c + 1) * 512],
                           kt == 0, kt == 15, [wb, b_hT[kt]], [bps])
                    dst = stg[si][:, c * 512:(c + 1) * 512]
                    if post == "sig":
                        act(dst, ps[:, :], AF.Sigmoid, [bps], [b_stg[si][c]])
                    elif post == "q":
                        ts("dve", dst, ps[:, :], QSCALE, None, ALU.mult, None, [bps], [b_stg[si][c]])
                    else:
                        evac(dst, ps[:, :], [bps], [b_stg[si][c]])
                    step_chain()
                r = row0 + ct * 128
                S.dma("sp", dst_d[r:r + 128, :], stg[si], b_stg[si], [], b_stg[si][0])
            mod_slab(8 + p2n[0])
            p2n[0] += 1

        def form1_slab(dst_d, col0):
            wslab, wb = wget()
            for ti in range(NT):
                si = sidx[0] % 3
                sidx[0] += 1
                ps, bps = next_ps(0, 7)
                for kt in range(16):
                    mm(ps[:, :], hT[:, kt, ti * 128:(ti + 1) * 128], wslab[:, kt, :], kt == 0, kt == 15, [wb, b_hT[kt]], [bps])
                evac(stg[si][:, 0:512], ps[:, :], [bps], [b_stg[si][0]])
                step_chain()
                S.dma("sp", dst_d[ti * 128:(ti + 1) * 128, col0:col0 + 512], stg[si][:, 0:512], [b_stg[si][0]], [], b_stg[si][0])
            mod_slab(8 + p2n[0])
            p2n[0] += 1

        for s in range(2):
            form2_slab(uT_d, s * 512, "copy")
        for s in range(2):
            form2_slab(qT_d, s * 512, "q")
        for s in range(2):
            form2_slab(kT_d, s * 512, "copy")
        for s in range(2):
            form1_slab(v_d, s * 512)
        for s in range(4):
            form2_slab(gaT_d, s * 512, "sig")
        for s in range(4):
            form2_slab(gbT_d, s * 512, "sig")
        mod_finish()
        S.barrier()
        if stop_after <= 2:
            S.barrier()
            S.emit()
            return nc

        o = 0
        bc = carve(o, [3, 64, 16], F32); o += 12288
        b_bc = S.buf("ssm_bc")
        S.dma("sp", bc.rearrange("p a b c -> p (a b c)"), ssm_bc_d, [], [b_bc], b_bc)
        T1 = carve(o, [64, 16], F32); o += 4096
        T2 = carve(o, [64, 16], F32); o += 4096
        Cs = carve(o, [64, 16], BF16); o += 2048
        o_setup_end = max(o, 32768)
        for _ in chain:
            pass

        def bc16(a):
            return apm(a, [a.ap[0], [1, 64], [0, 16]])

        b_T = S.buf("T12")
        tt("dve", T1, bc[:, 0], bc16(fre), ALU.mult, [b_bc, b_t], [b_T])
        tt("dve", T2, bc[:, 1], bc16(fims), ALU.mult, [b_bc, b_t], [b_T])
        tt("dve", T1, T1, T2, ALU.add, [b_T], [b_T])
        ts("dve", Cs, bc[:, 2], small[:, 153:154], None, ALU.mult, None, [b_bc, b_small], [b_T])
        o = o_setup_end
        BU = carve(o, [64, 128], BF16); o += 16384
        CPAD = carve(o, [64, 128], BF16); o += 16384
        b_BU = S.buf("BU")
        b_CP = S.buf("CPAD")
        for t8 in range(8):
            ps, bps = next_ps(0, 8)
            src = T1[:, t8 * 8:(t8 + 1) * 8, :].rearrange("p a b -> p (a b)")
            tr(ps[:, 0:128], src, identf, [b_T, b_cmat], [bps])
            tb = carve(o, [128], F32)
            b_tb = b_T
            cp("act", tb, ps[:, 0:128], [bps], [b_tb])
            for gl in range(8):
                ts("dve" if gl % 2 else "pool", BU[:, t8 * 8 + gl, :], tb, small[:, 160 + gl:161 + gl], None, ALU.mult, None,
                   [b_tb, b_small], [b_BU])
        S.op("pool", lambda e: e.memset(CPAD.rearrange("p a b -> p (a b)"), 0.0), [], [b_CP])
        CP4 = CPAD.rearrange("p (t g) c -> p t g c", g=8)
        Cs4 = Cs.rearrange("p (t g) c -> p t g c", g=8)
        for gl in range(8):
            cp("pool", CP4[:, :, gl, 16 * gl:16 * gl + 16], Cs4[:, :, gl, :], [b_T], [b_CP])
        S.barrier()

        o += 512
        uTs = [carve(o, [L], BF16), carve(o + 4096, [L], BF16)]; o += 8192
        b_uT = S.bufs(2, "uT")
        AM = [carve(o + i * 4096, [16, 128], BF16) for i in range(3)]; o += 12288
        b_AM = S.bufs(3, "AM")
        tAs = [carve(o, [16, 128], BF16)]; o += 4096
        tBs = [carve(o, [16, 128], BF16)]; o += 4096
        b_tA = S.bufs(1, "tA")
        b_tB = S.bufs(1, "tB")
        Sb = [carve(o + i * 4096, [L], BF16) for i in range(6)]; o += 24576
        b_Sb = [S.bufs(4, "Sb%d_" % i) for i in range(6)]
        vtmp = [carve(o + i * 1024, [512], BF16) for i in range(4)]; o += 4096
        b_vt = S.bufs(4, "vtmp")
        ytmp = [carve(o + i * 2048, [512], F32) for i in range(1)]; o += 2048
        b_yt = S.bufs(1, "ytmp")
        o_scan_end = o
        yT = carve(0, [8, L], BF16)
        assert 32768 <= o_setup_end
        b_yT = S.buf("yT")
        dT = small[:, 128:136]
        bglu = small[:, 136:144]

        def bcI(m):
            return apm(m, [m.ap[0], [0, 16], [1, 128]])

        def bcP(pw, g):
            a = pw[:, :, g:g + 1]
            return apm(a, [a.ap[0], [64, 16], [0, 128]])

        YP = [4, 5, 6, 7]
        pcnt = [0]

        def group_gen(g, par):
            t8, gl = g // 8, g % 8
            if gl == 0:
                S.dma("sp", uTs[t8 % 2], uT_d[t8 * 128:(t8 + 1) * 128, :], [], [b_uT[t8 % 2]], b_uT[t8 % 2])
            u_sb, b_u = uTs[t8 % 2], b_uT[t8 % 2]
            am, b_am = AM[par], b_AM[par]
            tt("dve", tAs[0], bcI(identf), bcP(pwre, g), ALU.mult, [b_cmat, b_pw], [b_tA[0]])
            tt("pool", tBs[0], bcI(swapf), bcP(pwim, g), ALU.mult, [b_cmat, b_pw], [b_tB[0]])
            tt("dve", am, tAs[0], tBs[0], ALU.add, [b_tA[0], b_tB[0]], [b_am])
            yield
            cur = par * 2
            nxt = par * 2 + 1
            for c in range(4):
                ps, bps = next_ps(0, 4)
                mm(ps[:, :], BU[:, g, :], u_sb[:, c * 512:(c + 1) * 512], True, True, [b_BU, b_u], [bps])
                evac(Sb[cur][:, c * 512:(c + 1) * 512], ps[:, :], [bps], [b_Sb[cur][c]])
                yield
            for (dd, ms) in LEVELS:
                for c in range(4):
                    c0 = c * 512
                    terms = []
                    for m in ms:
                        shf = m * dd
                        if shf >= c0 + 512:
                            continue
                        lo = max(0, shf - c0)
                        n = 512 - lo
                        s0 = c0 + lo - shf
                        rb_ = [b_Sb[cur][k] for k in range(s0 // 512, (s0 + n - 1) // 512 + 1)]
                        terms.append((am[:, PWS.index(shf), :], lo, s0, n, rb_))
                    pcnt[0] += 1
                    dst = Sb[nxt][:, c0:c0 + 512]
                    if not terms:
                        cp("pool", dst, Sb[cur][:, c0:c0 + 512], [b_Sb[cur][c]], [b_Sb[nxt][c]])
                    elif pcnt[0] % 2 == 0:
                        ps, bps = next_ps(0, 4)
                        mm(ps[:, :], ident[:], Sb[cur][:, c0:c0 + 512], True, False, [b_ident, b_Sb[cur][c]], [bps])
                        for i, (lh, lo, s0, n, rb_) in enumerate(terms):
                            mm(ps[:, lo:lo + n], lh, Sb[cur][:, s0:s0 + n], False, i == len(terms) - 1, [b_am] + rb_, [bps])
                        cp("act", dst, ps[:, :], [bps], [b_Sb[nxt][c]])
                    else:
                        ps, bps = next_ps(0, 4)
                        for i, (lh, lo, s0, n, rb_) in enumerate(terms):
                            mm(ps[:, lo:lo + n], lh, Sb[cur][:, s0:s0 + n], i == 0, i == len(terms) - 1, [b_am] + rb_, [bps])
                        lo0 = terms[0][1]
                        if lo0 > 0:
                            cp("pool", Sb[nxt][:, c0:c0 + lo0], Sb[cur][:, c0:c0 + lo0], [b_Sb[cur][c]], [b_Sb[nxt][c]])
                        tt("dve", Sb[nxt][:, c0 + lo0:c0 + 512], ps[:, lo0:512], Sb[cur][:, c0 + lo0:c0 + 512], ALU.add,
                           [bps, b_Sb[cur][c]], [b_Sb[nxt][c]])
                    yield
                cur, nxt = nxt, cur
            for c in range(4):
                mm(PS[YP[c]][:, :], CPAD[:, g, :], Sb[cur][:, c * 512:(c + 1) * 512], gl == 0, gl == 7,
                   [b_CP, b_Sb[cur][c]], [b_PS[YP[c]]])
            if gl == 7:
                for c in range(4):
                    cs_ = slice(c * 512, (c + 1) * 512)
                    stt(vtmp[c], u_sb[:, cs_], dT[:, t8:t8 + 1], PS[YP[c]][:, :], ALU.mult, ALU.add, [b_u, b_small, b_PS[YP[c]]], [b_vt[c]])
            yield
            if gl == 7:
                for c in range(4):
                    cs_ = slice(c * 512, (c + 1) * 512)
                    v_, bv = vtmp[c], b_vt[c]
                    w_, bw = ytmp[0], b_yt[0]
                    act(w_, v_, AF.Square, [bv], [bw])
                    ts("dve", w_, w_, 0.044715, 1.0, ALU.mult, ALU.add, [bw], [bw])
                    tt("pool", w_, w_, v_, ALU.mult, [bw, bv], [bw])
                    act(w_, w_, AF.Sigmoid, [bw], [bw], scale=1.5957691216057308)
                    tt("dve", yT[:, t8, cs_], v_, w_, ALU.mult, [bv, bw], [b_yT])
                    yield

        pending = list(range(64))
        active = []
        free_slots = [0, 1, 2]
        while pending or active:
            while pending and free_slots:
                g = pending.pop(0)
                slot = free_slots.pop(0)
                active.append((group_gen(g, slot), slot))
            for item in list(active):
                try:
                    next(item[0])
                except StopIteration:
                    active.remove(item)
                    free_slots.append(item[1])
        S.barrier()
        if dbg:
            for t8 in range(8):
                S.dma("sp", dbg_yssm[t8 * 128:(t8 + 1) * 128, :], yT[:, t8, :], [b_yT], [], b_yT)

        o = 32768
        ygT = carve(o, [8, L], BF16); o += 32768
        b_yg = S.buf("ygT")
        gsb = [carve(o, [L], BF16), carve(o + 4096, [L], BF16)]; o += 8192
        b_gsb = S.bufs(2, "gsb")
        sg = [carve(o, [512], BF16), carve(o + 1024, [512], BF16)]; o += 2048
        b_sg = S.bufs(2, "sg")
        mst = [carve(o, [L], BF16), carve(o + 4096, [L], BF16)]; o += 8192
        b_mst = S.bufs(2, "mst")
        n_ = 0
        for ct in range(8):
            if ct % 4 == 0:
                wslab, wb = wget()
            for c in range(4):
                cs_ = slice(c * 512, (c + 1) * 512)
                ps, bps = next_ps(0, 8)
                for kt in range(8):
                    mm(ps[:, :], wslab[:, kt, (ct % 4) * 128:(ct % 4 + 1) * 128], yT[:, kt, cs_], kt == 0, kt == 7, [wb, b_yT], [bps])
                i2 = n_ % 2
                n_ += 1
                act(sg[i2], ps[:, :], AF.Sigmoid, [bps, b_small], [b_sg[i2]], bias=bglu[:, ct:ct + 1])
                tt("dve", ygT[:, ct, cs_], sg[i2], yT[:, ct, cs_], ALU.mult, [b_sg[i2], b_yT], [b_yg])
        for half in range(4):
            wslab, wb = wget()
            for fl in range(4):
                ft = half * 4 + fl
                i2 = ft % 2
                S.dma("sp", gsb[i2], gaT_d[ft * 128:(ft + 1) * 128, :], [], [b_gsb[i2]], b_gsb[i2])
                for c in range(4):
                    cs_ = slice(c * 512, (c + 1) * 512)
                    ps, bps = next_ps(0, 8)
                    for kt in range(8):
                        mm(ps[:, :], wslab[:, kt, fl * 128:(fl + 1) * 128], ygT[:, kt, cs_], kt == 0, kt == 7, [wb, b_yg], [bps])
                    tt("dve", mst[i2][:, cs_], ps[:, :], gsb[i2][:, cs_], ALU.mult, [bps, b_gsb[i2]], [b_mst[i2]])
                S.dma("sp", mssm_d[ft * 128:(ft + 1) * 128, :], mst[i2], [b_mst[i2]], [], b_mst[i2])
        S.barrier()
        bcast_gates()
        if stop_after <= 3:
            S.barrier()
            S.emit()
            return nc

        yatt = carve(98304, [NT, 1024], BF16)
        b_yatt = S.buf("yatt")
        o = 0
        qh = [carve(o, [L], BF16), carve(o + 4096, [L], BF16)]; o += 8192
        kh = [carve(o, [L], BF16), carve(o + 4096, [L], BF16)]; o += 8192
        vh = [carve(o, [NT, 128], BF16), carve(o + 4096, [NT, 128], BF16)]; o += 8192
        bh = [carve(o, [768], F32), carve(o + 3072, [768], F32)]; o += 6144
        b_qh = S.bufs(2, "qh"); b_kh = S.bufs(2, "kh"); b_vh = S.bufs(2, "vh"); b_bh = S.bufs(2, "bh")
        cmk = carve(o, [512], F32); o += 2048
        b_cmk = S.buf("cmk")
        S.dma("sp", cmk, cmask_d, [], [b_cmk], b_cmk)
        Ssb = [carve(o, [L], F32), carve(o + 8192, [L], F32)]; o += 16384
        b_Ssb = [S.bufs(8, "Ssb%d_" % i) for i in range(2)]
        Pb = [carve(o, [L], BF16), carve(o + 4096, [L], BF16)]; o += 8192
        b_Pb = S.bufs(2, "Pb")
        PT = [carve(o, [16, 128], BF16), carve(o + 4096, [16, 128], BF16)]; o += 8192
        b_PT = [S.bufs(2, "PT%d_" % i) for i in range(2)]
        kms = carve(o, [8], F32); o += 32
        kmb = carve(o, [8], BF16); o += 32
        b_km = S.buf("km")
        NSM = 8
        smv = [carve(o + i * 128, [32], F32) for i in range(NSM)]; o += NSM * 128
        b_smv = S.bufs(NSM, "smv")
        gsbuf = carve(o, [8], F32); o += 32
        b_gs = S.buf("gatesb")
        c31 = small[:, 144:152]
        vT_view = v_d.rearrange("(t p) c -> p t c", p=128)

        def load_head(h):
            i2 = h % 2
            S.dma("sp", qh[i2], qT_d[h * 128:(h + 1) * 128, :], [], [b_qh[i2]], b_qh[i2])
            S.dma("sp", kh[i2], kT_d[h * 128:(h + 1) * 128, :], [], [b_kh[i2]], b_kh[i2])
            S.dma("sp", vh[i2], vT_view[:, :, h * 128:(h + 1) * 128], [], [b_vh[i2]], b_vh[i2])
            S.dma("sp", bh[i2], bias_t_d[h], [], [b_bh[i2]], b_bh[i2])
            tt("pool", bh[i2][:, 0:512], bh[i2][:, 0:512], cmk, ALU.add, [b_bh[i2], b_cmk], [b_bh[i2]])

        kms2 = [kms, carve(o, [8], F32)]; o += 32
        kmb2 = [kmb, carve(o, [8], BF16)]; o += 32
        gs2 = [gsbuf, carve(o, [8], F32)]; o += 32
        b_km2 = [b_km, S.buf("km1")]
        b_gs2 = [b_gs, S.buf("gs1")]

        def stream_gen(st_):
            sb0 = 4 * st_
            pt_i = sb0 + 2
            ob = sb0 + 3
            kms_, kmb_, gsb_, bkm, bgs = kms2[st_], kmb2[st_], gs2[st_], b_km2[st_], b_gs2[st_]
            q_, k_, v_, bt_ = qh[st_], kh[st_], vh[st_], bh[st_]
            bq, bk, bv, bb = b_qh[st_], b_kh[st_], b_vh[st_], b_bh[st_]
            Sd, bS = Ssb[st_], b_Ssb[st_]
            Pd, bP = Pb[st_], b_Pb[st_]
            PTd, bPT = PT[st_], b_PT[st_]
            n = 0
            for h in range(4 * st_, 4 * st_ + 4):
                S.dma("sp", q_, qT_d[h * 128:(h + 1) * 128, :], [], [bq], bq)
                S.dma("sp", k_, kT_d[h * 128:(h + 1) * 128, :], [], [bk], bk)
                S.dma("sp", v_, vT_view[:, :, h * 128:(h + 1) * 128], [], [bv], bv)
                S.dma("sp", bt_, bias_t_d[h], [], [bb], bb)
                tt("pool", bt_[:, 0:512], bt_[:, 0:512], cmk, ALU.add, [bb, b_cmk], [bb])
                S.op("dve", lambda e, k_=k_, kms_=kms_: e.tensor_reduce(out=kms_, in_=k_.rearrange("p (a b) -> p a b", b=256), axis=AX.X, op=ALU.add),
                     [bk], [bkm])
                ts("dve", kmb_, kms_, 1.0 / 256.0, None, ALU.mult, None, [bkm], [bkm])
                S.op("pool", lambda e, gsb_=gsb_: e.memset(gsb_, -1e30), [], [bgs])
                yield
                for i in range(NT):
                    own, var = i // 2, i % 2
                    nk = 256 * own + 128 * (var + 1)
                    nch = nk // 128
                    n += 1
                    sv, bsv = smv[(2 * n + st_) % NSM], b_smv[(2 * n + st_) % NSM]
                    qs = q_[:, i * 128:(i + 1) * 128]
                    sel = own >= 4
                    if sel:
                        mm(PS[ob][:, 128:136], qs, kmb_, True, True, [bq, bkm], [b_PS[ob]])
                        cp("dve", gsb_[:, 0:own], PS[ob][:, 128:128 + own], [b_PS[ob]], [bgs])
                        S.op("dve", lambda e, sv=sv, gsb_=gsb_: e.max(out=sv[:, 0:8], in_=gsb_), [bgs], [bsv])
                        ts("dve", sv[:, 8:8 + own], gsb_[:, 0:own], sv[:, 2:3], -NEG, ALU.is_ge, ALU.mult, [bgs, bsv], [bsv])
                        ts("dve", sv[:, 16:16 + own], sv[:, 8:8 + own], NEG, c31[:, h:h + 1], ALU.add, ALU.add, [bsv, b_small], [bsv])
                        ts("dve", sv[:, 24:24 + own], sv[:, 8:8 + own], NEG, None, ALU.add, None, [bsv], [bsv])
                    for r0 in range(0, nk, 1024):
                        r1 = min(nk, r0 + 1024)
                        for bnk in range((r1 - r0 + 511) // 512):
                            c0 = r0 + bnk * 512
                            w_ = min(512, r1 - c0)
                            mm(PS[sb0 + bnk][:, 0:w_], qs, k_[:, c0:c0 + w_], True, True, [bq, bk], [b_PS[sb0 + bnk]])
                        yield
                        for kb in range(r0 // 256, (r1 + 255) // 256):
                            bnk, off = sb0 + (kb * 256 - r0) // 512, ((kb * 256 - r0) % 512)
                            src = PS[bnk][:, off:off + 256]
                            dst = Sd[:, kb * 256:(kb + 1) * 256]
                            rd = [b_PS[bnk]]
                            if kb == own:
                                w_ = 128 * (var + 1)
                                tt("dve", Sd[:, kb * 256:kb * 256 + w_], PS[bnk][:, off:off + w_], bt_[:, var * 256:var * 256 + w_],
                                   ALU.add, rd + [bb], [bS[kb]])
                            elif kb == own - 1 and var == 0:
                                if sel:
                                    stt(dst, src, sv[:, 24 + kb:25 + kb], bt_[:, 512:768], ALU.add, ALU.add, rd + [bsv, bb], [bS[kb]])
                                else:
                                    tt("dve", dst, src, bt_[:, 512:768], ALU.add, rd + [bb], [bS[kb]])
                            else:
                                col = sv[:, 16 + kb:17 + kb] if sel else c31[:, h:h + 1]
                                kb2 = kb ^ 1
                                special = (kb2 == own) or (kb2 == own - 1 and var == 0)
                                if (not special) and (kb // 2) % 2 == 0:
                                    act(dst, src, AF.Identity, rd + [bsv, b_small], [bS[kb]], bias=col)
                                else:
                                    ts("dve", dst, src, col, None, ALU.add, None, rd + [bsv, b_small], [bS[kb]])
                        yield
                    S.op("dve", lambda e, Sd=Sd, nk=nk, sv=sv: e.tensor_reduce(out=sv[:, 3:4], in_=Sd[:, 0:nk], axis=AX.X, op=ALU.max, negate=True),
                         bS[0:own + 1], [bsv])
                    yield
                    act(Pd[:, 0:nk], Sd[:, 0:nk], AF.Exp, bS[0:own + 1] + [bsv], [bP, bsv], bias=sv[:, 3:4], accum=sv[:, 4:5])
                    S.op("dve", lambda e, sv=sv: e.reciprocal(out=sv[:, 5:6], in_=sv[:, 4:5]), [bsv], [bsv])
                    yield
                    for b8 in range((nch + 7) // 8):
                        psb = PS[pt_i][:, :].bitcast(BF16)
                        n8 = min(8, nch - b8 * 8)
                        for c in range(n8):
                            cc = b8 * 8 + c
                            tr(psb[:, c * 128:(c + 1) * 128], Pd[:, cc * 128:(cc + 1) * 128], ident[:], [bP, b_ident], [b_PS[pt_i]])
                        yield
                        evac(PTd[:, b8 * 8:b8 * 8 + n8, :].rearrange("p a b -> p (a b)"), psb[:, 0:n8 * 128], [b_PS[pt_i]], [bPT[b8]])
                        yield
                    for cc in range(nch):
                        mm(PS[ob][:, 0:128], PTd[:, cc, :], v_[:, cc, :], cc == 0, cc == nch - 1, [bPT[cc // 8], bv], [b_PS[ob]])
                    yield
                    act(yatt[:, i, h * 128:(h + 1) * 128], PS[ob][:, 0:128], AF.Copy, [b_PS[ob], bsv], [b_yatt], scale=sv[:, 5:6])
                    yield

        gens = [stream_gen(0), stream_gen(1)]
        while gens:
            for gen in list(gens):
                try:
                    next(gen)
                except StopIteration:
                    gens.remove(gen)
        S.barrier()
        if dbg:
            S.dma("sp", dbg_yatt.rearrange("(t p) c -> p t c", p=128), yatt, [b_yatt], [], b_yatt)

        yaT = carve(0, [8, L], BF16)
        b_yaT = S.buf("yaT")
        mT = carve(32768, [16, L], BF16)
        b_mT = S.buf("mT")
        o = 98304
        gsb = [carve(o, [L], BF16), carve(o + 4096, [L], BF16)]; o += 8192
        msb = [carve(o, [L], BF16), carve(o + 4096, [L], BF16)]; o += 8192
        b_gsb = S.bufs(2, "gsb2"); b_msb = S.bufs(2, "msb")
        tq_ = [carve(o, [512], F32), carve(o + 2048, [512], F32)]; o += 4096
        b_tq = S.bufs(2, "tq")
        for i in range(NT):
            ps, bps = next_ps(0, 8)
            psb = ps[:, :].bitcast(BF16)
            for h in range(8):
                tr(psb[:, h * 128:(h + 1) * 128], yatt[:, i, h * 128:(h + 1) * 128], ident[:], [b_yatt, b_ident], [bps])
            evac(yaT[:, :, i * 128:(i + 1) * 128], psb.rearrange("p (a b) -> p a b", b=128), [bps], [b_yaT])
        S.barrier()
        n_ = 0
        for half in range(4):
            wslab, wb = wget()
            for fl in range(4):
                ft = half * 4 + fl
                i2 = ft % 2
                S.dma("sp", gsb[i2], gbT_d[ft * 128:(ft + 1) * 128, :], [], [b_gsb[i2]], b_gsb[i2])
                S.dma("sp", msb[i2], mssm_d[ft * 128:(ft + 1) * 128, :], [], [b_msb[i2]], b_msb[i2])
                for c in range(4):
                    cs_ = slice(c * 512, (c + 1) * 512)
                    ps, bps = next_ps(0, 8)
                    for kt in range(8):
                        mm(ps[:, :], wslab[:, kt, fl * 128:(fl + 1) * 128], yaT[:, kt, cs_], kt == 0, kt == 7, [wb, b_yaT], [bps])
                    j2 = n_ % 2
                    n_ += 1
                    tt("dve", tq_[j2], ps[:, :], gsb[i2][:, cs_], ALU.mult, [bps, b_gsb[i2]], [b_tq[j2]])
                    tt("dve", mT[:, ft, cs_], tq_[j2], msb[i2][:, cs_], ALU.add, [b_tq[j2], b_msb[i2]], [b_mT])
        S.barrier()

        o = 0
        xin = [carve(o + i * 2048, [512], F32) for i in range(3)]; o += 6144
        xo = [carve(o + i * 2048, [512], F32) for i in range(3)]; o += 6144
        b_xin = S.bufs(3, "xin"); b_xo = S.bufs(3, "xo")
        assert o <= 32768
        n_ = 0
        for cs4 in range(4):
            wslab, wb = wget()
            csl = slice(cs4 * 512, (cs4 + 1) * 512)
            for ti in range(NT):
                j3 = n_ % 3
                n_ += 1
                S.dma("sp", xin[j3], x_d[ti * 128:(ti + 1) * 128, csl], [], [b_xin[j3]], b_xin[j3])
                ps, bps = next_ps(0, 8)
                for kt in range(16):
                    mm(ps[:, :], mT[:, kt, ti * 128:(ti + 1) * 128], wslab[:, kt, :], kt == 0, kt == 15, [wb, b_mT], [bps])
                tt("dve", xo[j3], ps[:, :], gbc[:, 0, csl], ALU.mult, [bps, b_gbc[0]], [b_xo[j3]])
                tt("dve", xo[j3], xo[j3], xin[j3], ALU.add, [b_xo[j3], b_xin[j3]], [b_xo[j3]])
                S.dma("sp", x1_d[ti * 128:(ti + 1) * 128, csl], xo[j3], [b_xo[j3]], [], b_xo[j3])
        S.barrier()
        if stop_after <= 5:
            S.barrier()
            S.emit()
            return nc

        o = 0
        X1 = [carve(o + i * 8192, [D], F32) for i in range(4)]; o += 32768
        b_X1 = S.bufs(4, "X1")
        h2T = carve(o, [16, 512], BF16); o += 16384
        b_h2 = S.bufs(16, "h2T")
        fT = carve(o, [64, 512], BF16); o += 65536
        b_fT = S.buf("fT")
        rt = [carve(o, [512], BF16), carve(o + 1024, [512], BF16)]; o += 2048
        b_rt = S.bufs(2, "rt")
        junk = carve(o, [D], BF16); o += 4096
        b_junk = S.buf("junk2")
        xn = [carve(o, [D], BF16), carve(o + 4096, [D], BF16)]; o += 8192
        b_xn = S.bufs(2, "xn2")
        tq_ = [carve(o, [512], F32), carve(o + 2048, [512], F32)]; o += 4096
        b_tq = S.bufs(2, "tq2")
        S.dma("sp", gbc[:, 0, :], gfin_d, [], [b_gbc[0]], b_gbc[0])
        stat2 = misc[:, 224:480]
        b_stat2 = S.bufs(32, "stat2")
        n_ = 0
        for tc in range(4):
            for j in range(4):
                ti = tc * 4 + j
                S.dma("sp", X1[j], x1_d[ti * 128:(ti + 1) * 128, :], [], [b_X1[j]], b_X1[j])
            for j in range(4):
                ti = tc * 4 + j
                sc = stat2[:, 4 * ti:4 * ti + 4]
                norm_tile(X1[j], b_X1[j], ti, xn[j % 2], b_xn[j % 2], sc, b_stat2[ti])
                transpose_mod(xn[j % 2], b_xn[j % 2], h2T, b_h2, slice(j * 128, (j + 1) * 128), A2, sh2, b_modT2)
            for fs in range(16):
                wslab, wb = wget()
                for jj in range(4):
                    fi = fs * 4 + jj
                    ps, bps = next_ps(0, 4)
                    for kt in range(16):
                        mm(ps[:, :], wslab[:, kt, jj * 128:(jj + 1) * 128], h2T[:, kt, :], kt == 0, kt == 15, [wb, b_h2[kt]], [bps])
                    j2 = n_ % 2
                    n_ += 1
                    act(rt[j2], ps[:, :], AF.Relu, [bps], [b_rt[j2]])
                    tt("dve", fT[:, fi, :], rt[j2], rt[j2], ALU.mult, [b_rt[j2]], [b_fT])
            for cs4 in range(4):
                csl = slice(cs4 * 512, (cs4 + 1) * 512)
                base = 4 if cs4 % 2 == 0 else 0
                for kg in range(4):
                    wslab, wb = wget()
                    for kk in range(16):
                        kt = kg * 16 + kk
                        for j in range(4):
                            mm(PS[base + j][:, :], fT[:, kt, j * 128:(j + 1) * 128], wslab[:, kk, :], kt == 0, kt == 63,
                               [wb, b_fT], [b_PS[base + j]])
                for j in range(4):
                    j2 = n_ % 2
                    n_ += 1
                    tt("dve", tq_[j2], PS[base + j][:, :], gbc[:, 1, csl], ALU.mult, [b_PS[base + j], b_gbc[1]], [b_tq[j2]])
                    tt("dve", X1[j][:, csl], X1[j][:, csl], tq_[j2], ALU.add, [b_X1[j], b_tq[j2]], [b_X1[j]])
            for j in range(4):
                ti = tc * 4 + j
                sc = stat2[:, 64 + 4 * ti:64 + 4 * ti + 4]
                bst = b_stat2[16 + ti]
                act(junk, X1[j], AF.Square, [b_X1[j]], [b_junk, bst], accum=sc[:, 0:1])
                ts("dve", sc[:, 1:2], sc[:, 0:1], 1.0 / D, EPS, ALU.mult, ALU.add, [bst], [bst])
                act(sc[:, 2:3], sc[:, 1:2], AF.Sqrt, [bst], [bst])
                S.op("dve", lambda e, sc=sc: e.reciprocal(out=sc[:, 3:4], in_=sc[:, 2:3]), [bst], [bst])
                stt(X1[j], X1[j], sc[:, 3:4], gbc[:, 0, :], ALU.mult, ALU.mult, [b_X1[j], bst, b_gbc[0]], [b_X1[j]])
                S.dma("sp", out_d[ti * 128:(ti + 1) * 128, :], X1[j], [b_X1[j]], [], b_X1[j])
        S.barrier()
        S.emit()
    return nc


_CACHE = {}


def kernel(**inputs):
    inp = {k: np.asarray(v) for k, v in inputs.items()}
    if "nc" not in _CACHE:
        _CACHE["nc"] = build_program()
    nc = _CACHE["nc"]
    sh = _host_shared(inp)
    in_maps = []
    for b in range(8):
        m = dict(sh)
        m.update(_host_core(inp, b))
        in_maps.append(m)
    res = run_bass_kernel_spmd(nc, in_maps, core_ids=list(range(8)))
    out = np.stack([np.asarray(r["out"]) for r in res.results], 0)
    return out.astype(np.float32)
```
